# Optimizing a Trainium2 kernel written in Bass

```python
import jax, jax.numpy as jnp
from jax import lax
import numpy as np

D_MODEL = 1024
BATCH = 1
SEQ = 16384
DEPTH = 4

CHUNK = 64
MEM_LEN = 256
N_A_LAYERS = DEPTH // 2
N_B_LAYERS = DEPTH - N_A_LAYERS
MEM_HEADS = 4
MEM_HEAD_DIM = 64
MEM_W = MEM_HEADS * MEM_HEAD_DIM
MAIN_W = D_MODEL - MEM_W
CONV_W = MAIN_W
CONV_K = 3
DIFF_HEAD_DIM = 64
DIFF_HEADS = MAIN_W // (2 * DIFF_HEAD_DIM)
DIFF_QK = DIFF_HEADS * DIFF_HEAD_DIM
ROPE_DIM = DIFF_HEAD_DIM // 4
ROPE_THETA = 500000.0
D_FF = 4 * D_MODEL
Q_BLOCK = 128
EPS = 1e-6
SUBLN_EPS = 1e-5

kernel_name = "yoco_shortconv_diffattn_memory_trunk"


def rmsnorm(x, g, eps=EPS):
    xf = x.astype(jnp.float32)
    y = xf * lax.rsqrt(jnp.mean(xf * xf, axis=-1, keepdims=True) + eps)
    return (y * g.astype(jnp.float32)).astype(x.dtype)


def rope_tables(seq):
    inv = 1.0 / (ROPE_THETA ** (jnp.arange(0, ROPE_DIM, 2, dtype=jnp.float32) / ROPE_DIM))
    ang = jnp.arange(seq, dtype=jnp.float32)[:, None] * inv[None, :]
    return jnp.cos(ang), jnp.sin(ang)


def partial_rope(x, cos, sin):
    half = ROPE_DIM // 2
    c = cos[None, :, None, :].astype(x.dtype)
    s = sin[None, :, None, :].astype(x.dtype)
    x1, x2, xp = x[..., :half], x[..., half:ROPE_DIM], x[..., ROPE_DIM:]
    return jnp.concatenate([x1 * c - x2 * s, x1 * s + x2 * c, xp], axis=-1)


def short_conv_mixer(h, gate_b, gate_c, w_conv):
    u = gate_c * h
    s_len = u.shape[1]
    up = jnp.pad(u, ((0, 0), (CONV_K - 1, 0), (0, 0)))
    y = w_conv[0] * up[:, 0:s_len]
    for k in range(1, CONV_K):
        y = y + w_conv[k] * up[:, k:k + s_len]
    return gate_b * y


def memory_attention(q, mem_n, w_mem_kv, q_gain, k_gain):
    b, s, _ = q.shape
    m = mem_n.shape[1]
    kv = jnp.einsum('bmd,de->bme', mem_n, w_mem_kv)
    k = kv[..., :MEM_W].reshape(b, m, MEM_HEADS, MEM_HEAD_DIM)
    v = kv[..., MEM_W:].reshape(b, m, MEM_HEADS, MEM_HEAD_DIM)
    qh = rmsnorm(q.reshape(b, s, MEM_HEADS, MEM_HEAD_DIM), q_gain)
    k = rmsnorm(k, k_gain)
    sc = jnp.einsum('bshd,bmhd->bhsm', qh, k).astype(jnp.float32) * (MEM_HEAD_DIM ** -0.5)
    p = jax.nn.softmax(sc, axis=-1).astype(v.dtype)
    o = jnp.einsum('bhsm,bmhd->bshd', p, v)
    return o.reshape(b, s, MEM_W)


def diff_attention(q1, q2, k1, k2, v, lam):
    b, s, h, dh = q1.shape
    nblk = s // Q_BLOCK
    scale = dh ** -0.5
    key_chunk = jnp.arange(s) // CHUNK
    qb1 = jnp.moveaxis(q1.reshape(b, nblk, Q_BLOCK, h, dh), 1, 0)
    qb2 = jnp.moveaxis(q2.reshape(b, nblk, Q_BLOCK, h, dh), 1, 0)

    def one_block(args):
        a1, a2, i = args
        q_chunk = (i * Q_BLOCK + jnp.arange(Q_BLOCK)) // CHUNK
        mask = (key_chunk[None, :] <= q_chunk[:, None])[None, None]
        s1 = jnp.einsum('bqhd,bkhd->bhqk', a1, k1).astype(jnp.float32) * scale
        s2 = jnp.einsum('bqhd,bkhd->bhqk', a2, k2).astype(jnp.float32) * scale
        p1 = jax.nn.softmax(jnp.where(mask, s1, -jnp.inf), axis=-1)
        p2 = jax.nn.softmax(jnp.where(mask, s2, -jnp.inf), axis=-1)
        p = (p1 - lam * p2).astype(v.dtype)
        return jnp.einsum('bhqk,bkhe->bqhe', p, v)

    out = lax.map(one_block, (qb1, qb2, jnp.arange(nblk)))
    return jnp.moveaxis(out, 0, 1).reshape(b, s, h, 2 * dh)


def sqrelu_mlp(h, w_up, w_down):
    u = jnp.einsum('bsd,df->bsf', h, w_up)
    return jnp.einsum('bsf,fd->bsd', jnp.square(jax.nn.relu(u)), w_down)


def setup_inputs(seed: int = 0) -> dict:
    key = jax.random.key(seed)
    ks = jax.random.split(key, 24)
    f32 = jnp.float32
    nrm = lambda k, shp, sc: jax.random.normal(k, shp, f32) * sc
    gain = lambda k, shp: 1.0 + 0.02 * jax.random.normal(k, shp, f32)
    return {
        "x": nrm(ks[0], (BATCH, SEQ, D_MODEL), 1.0),
        "mem": nrm(ks[1], (BATCH, MEM_LEN, D_MODEL), 1.0),
        "norm_mix": gain(ks[2], (DEPTH, D_MODEL)),
        "norm_mlp": gain(ks[3], (DEPTH, D_MODEL)),
        "a_w_in": nrm(ks[4], (N_A_LAYERS, D_MODEL, 3 * CONV_W + MEM_W), D_MODEL ** -0.5),
        "a_conv": nrm(ks[5], (N_A_LAYERS, CONV_K, CONV_W), CONV_K ** -0.5),
        "b_w_q": nrm(ks[6], (N_B_LAYERS, D_MODEL, 2 * DIFF_QK + MEM_W), D_MODEL ** -0.5),
        "b_q_norm": gain(ks[7], (N_B_LAYERS, DIFF_HEAD_DIM)),
        "b_lam": nrm(ks[8], (N_B_LAYERS, 4, DIFF_HEAD_DIM), 0.1),
        "b_subln": gain(ks[9], (N_B_LAYERS, 2 * DIFF_HEAD_DIM)),
        "kv_norm": gain(ks[10], (D_MODEL,)),
        "w_kv": nrm(ks[11], (D_MODEL, 2 * DIFF_QK + MAIN_W), D_MODEL ** -0.5),
        "k_norm": gain(ks[12], (DIFF_HEAD_DIM,)),
        "mem_norm": gain(ks[13], (D_MODEL,)),
        "w_mem_kv": nrm(ks[14], (DEPTH, D_MODEL, 2 * MEM_W), D_MODEL ** -0.5),
        "mem_q_norm": gain(ks[15], (DEPTH, MEM_HEAD_DIM)),
        "mem_k_norm": gain(ks[16], (DEPTH, MEM_HEAD_DIM)),
        "w_o": nrm(ks[17], (DEPTH, MAIN_W + MEM_W, D_MODEL), (MAIN_W + MEM_W) ** -0.5),
        "w_up": nrm(ks[18], (DEPTH, D_MODEL, D_FF), D_MODEL ** -0.5),
        "w_down": nrm(ks[19], (DEPTH, D_FF, D_MODEL), 0.5 * D_FF ** -0.5),
    }


def reference(x, mem, norm_mix, norm_mlp, a_w_in, a_conv, b_w_q, b_q_norm, b_lam, b_subln,
              kv_norm, w_kv, k_norm, mem_norm, w_mem_kv, mem_q_norm, mem_k_norm,
              w_o, w_up, w_down):
    b, s, _ = x.shape
    cos, sin = rope_tables(s)
    mem_n = rmsnorm(mem, mem_norm)
    k1 = k2 = v = None
    for l in range(DEPTH):
        h = rmsnorm(x, norm_mix[l])
        if l < N_A_LAYERS:
            proj = jnp.einsum('bsd,de->bse', h, a_w_in[l])
            gate_b = proj[..., :CONV_W]
            gate_c = proj[..., CONV_W:2 * CONV_W]
            hv = proj[..., 2 * CONV_W:3 * CONV_W]
            qm = proj[..., 3 * CONV_W:]
            main = short_conv_mixer(hv, gate_b, gate_c, a_conv[l])
        else:
            j = l - N_A_LAYERS
            proj = jnp.einsum('bsd,de->bse', h, b_w_q[j])
            q1 = proj[..., :DIFF_QK].reshape(b, s, DIFF_HEADS, DIFF_HEAD_DIM)
            q2 = proj[..., DIFF_QK:2 * DIFF_QK].reshape(b, s, DIFF_HEADS, DIFF_HEAD_DIM)
            qm = proj[..., 2 * DIFF_QK:]
            q1 = partial_rope(rmsnorm(q1, b_q_norm[j]), cos, sin)
            q2 = partial_rope(rmsnorm(q2, b_q_norm[j]), cos, sin)
            lam_init = 0.8 - 0.6 * float(np.exp(-0.3 * l))
            lp = b_lam[j].astype(jnp.float32)
            lam = (jnp.exp(jnp.sum(lp[0] * lp[1])) - jnp.exp(jnp.sum(lp[2] * lp[3]))
                   + lam_init)
            o = diff_attention(q1, q2, k1, k2, v, lam)
            o = rmsnorm(o, b_subln[j], SUBLN_EPS) * (1.0 - lam_init)
            main = o.reshape(b, s, MAIN_W)
        mo = memory_attention(qm, mem_n, w_mem_kv[l], mem_q_norm[l], mem_k_norm[l])
        x = x + jnp.einsum('bse,ed->bsd', jnp.concatenate([main, mo], axis=-1), w_o[l])
        x = x + sqrelu_mlp(rmsnorm(x, norm_mlp[l]), w_up[l], w_down[l])
        if l == N_A_LAYERS - 1:
            kvh = rmsnorm(x, kv_norm)
            kv = jnp.einsum('bsd,de->bse', kvh, w_kv)
            k1 = kv[..., :DIFF_QK].reshape(b, s, DIFF_HEADS, DIFF_HEAD_DIM)
            k2 = kv[..., DIFF_QK:2 * DIFF_QK].reshape(b, s, DIFF_HEADS, DIFF_HEAD_DIM)
            v = kv[..., 2 * DIFF_QK:].reshape(b, s, DIFF_HEADS, 2 * DIFF_HEAD_DIM)
            k1 = partial_rope(rmsnorm(k1, k_norm), cos, sin)
            k2 = partial_rope(rmsnorm(k2, k_norm), cos, sin)
    return x
```

```python
import numpy as np
import ml_dtypes
import concourse.bass as bass
import concourse.mybir as mybir
from concourse.bass_utils import run_bass_kernel_spmd

F32 = mybir.dt.float32
BF16 = mybir.dt.bfloat16
I32 = mybir.dt.int32
ALU = mybir.AluOpType
AF = mybir.ActivationFunctionType

NCORES = 8
D = 1024
SEQ = 16384
NCH = 8
T = 512
NTOK = 2048
NT = 4
BLK = 1024
MEM = 256
EPS = 1e-6
SUBLN_EPS = 1e-5
THETA = 500000.0
HS = 4096
NSLOT = 8
SAME_ENG_SYNC = True
DEBUG = None

CST = {}
_c = 0
def _add(name, n):
    global _c
    CST[name] = _c
    _c += n
for _l in range(4):
    _add(("norm_mix", _l), 8)
    _add(("norm_mlp", _l), 8)
_add("kv_norm", 8)
_add("mem_norm", 8)
for _l in range(2):
    for _k in range(3):
        _add(("conv", _l, _k), 6)
for _l in range(4):
    _add(("mem_q", _l), 1)
    _add(("mem_k", _l), 1)
for _j in range(2):
    _add(("b_q", _j), 1)
    _add(("subln", _j), 1)
_add("k_norm", 1)
_add("invf", 1)
_add("sgn", 1)
NCST = _c


class Sched:
    def __init__(self, nc):
        self.nc = nc
        self.order = ["pe", "act", "dve", "pool", "sp"]
        self.ops = {e: [] for e in self.order}
        self.state = {}
        self.nid = 0
        self.streams = {}

    def op(self, eng, fn, R=(), W=(), stream=None):
        o = {"eng": eng, "fn": fn, "deps": {}, "stream": stream, "id": self.nid, "used": False}
        self.nid += 1
        for k in R:
            st = self.state.get(k)
            if st is not None and st[0] is not None:
                o["deps"][st[0]["id"]] = st[0]
        for k in W:
            st = self.state.get(k)
            if st is not None:
                if st[0] is not None:
                    o["deps"][st[0]["id"]] = st[0]
                for r in st[1]:
                    o["deps"][r["id"]] = r
        for k in R:
            st = self.state.setdefault(k, [None, []])
            st[1].append(o)
        for k in W:
            self.state[k] = [o, []]
        self.ops[eng].append(o)
        return o

    def dma(self, eng, fn, R=(), W=(), stream="d"):
        return self.op(eng, fn, R, W, stream=stream)

    def finalize(self, sem_ctx):
        for e in self.order:
            for o in self.ops[e]:
                for d in o["deps"].values():
                    if d["stream"] is not None:
                        d["used"] = True
                    elif d["eng"] != o["eng"]:
                        d["used"] = True
                    elif SAME_ENG_SYNC and o["eng"] != "pe":
                        d["used"] = True
        engsem = {}
        for e in self.order:
            cnt = 0
            for o in self.ops[e]:
                if o["stream"] is not None:
                    s = self.streams.setdefault(o["stream"], [None, 0])
                    s[1] += 16
                    o["ticket"] = ("s", o["stream"], s[1])
                elif o["used"]:
                    cnt += 1
                    o["ticket"] = ("e", e, cnt)
        return

    def emit(self, block, sems):
        nc = self.nc
        handles = {"pe": block.tensor, "act": block.scalar, "dve": block.vector, "pool": block.gpsimd, "sp": block.sync}
        for e in self.order:
            ops = self.ops[e]
            if not ops:
                continue

            def body(h, ops=ops, e=e):
                seen = {}
                for o in ops:
                    for d in o["deps"].values():
                        if d["stream"] is None and d["eng"] == e and (e == "pe" or not SAME_ENG_SYNC):
                            continue
                        tk = d["ticket"]
                        key = tk[:2]
                        if seen.get(key, 0) >= tk[2]:
                            continue
                        seen[key] = tk[2]
                        h.wait_ge(sems[key], tk[2])
                    ins = o["fn"](h)
                    if o["stream"] is not None:
                        ins.then_inc(sems[("s", o["stream"])], 16)
                    elif o["used"]:
                        ins.then_inc(sems[("e", e)], 1)
                if e in ("sp", "pool"):
                    for o in ops:
                        pass
            handles[e](body)


def build_program(phase):
    nc = bass.Bass("TRN2", target_bir_lowering=False)
    S = Sched(nc)
    import contextlib
    es = contextlib.ExitStack()

    def dram(name, shape, dt, kind):
        return nc.dram_tensor(name, shape, dt, kind=kind).ap()

    def sb(name, shape, dt):
        return es.enter_context(nc.sbuf_tensor(name, shape, dt))

    def pst(name):
        return es.enter_context(nc.psum_tensor(name, [128, 512], F32))

    n_hs = 51 if phase == "A" else 42
    wts = dram("wts", [n_hs, 128, HS], F32, "ExternalInput")
    cst_d = dram("cst", [128, NCST], F32, "ExternalInput")
    memT_d = dram("memT", [128, 8 * MEM], F32, "ExternalInput")
    cmat_d = dram("cmat", [128, 4 * 128], F32, "ExternalInput")
    xin_d = dram("xin", [128, NCH * NTOK], F32, "ExternalInput")
    pos_d = dram("pos", [128, NTOK], F32, "ExternalInput")
    if phase == "A":
        xh_d = dram("xh", [128, NCH * 8], F32, "ExternalInput")
        hflag_d = dram("hflag", [128, 8], F32, "ExternalInput")
        xout_d = dram("xout", [128, NCH * NTOK], F32, "ExternalOutput")
        kT_d = dram("kT", [6, 128, NTOK], BF16, "ExternalOutput")
        v_d = dram("v", [NTOK, 768], BF16, "ExternalOutput")
    else:
        kTall_d = dram("kTall", [NCORES, 6, 128, NTOK], BF16, "ExternalInput")
        vall_d = dram("vall", [NCORES, NTOK, 768], BF16, "ExternalInput")
        lam_d = dram("lamb", [128, 512], F32, "ExternalInput")
        qc_d = dram("qc", [128, NTOK], F32, "ExternalInput")
        kc_d = dram("kc", [128, 128], F32, "ExternalInput")
        xout_d = dram("xout", [128, NCH * NTOK], F32, "ExternalOutput")

    x_t = sb("x", [128, NCH * NTOK], F32)
    NSL = NSLOT if phase == "A" else 6
    ring = [sb(f"w{i}", [128, HS], BF16) for i in range(NSL)]
    R_t = sb("R", [128, 16384], BF16)
    mm_t = sb("mainmo", [128, NCH * T], BF16)
    a_t = [sb(f"amlp{i}", [128, 4 * T], BF16) for i in range(2)]
    sq_t = [sb(f"sq{i}", [128, T], BF16) for i in range(2)]
    rs_t = [sb(f"rs{i}", [128, T], F32) for i in range(2)]
    sqf_t = [sb(f"sqf{i}", [128, T], F32) for i in range(2)]
    rd_t = sb("rd", [128, T], F32)
    pT_t = [sb(f"pT{i}", [128, T], BF16) for i in range(4)]
    cst = sb("cst_s", [128, NCST], F32)
    cmat_f = None
    cmat = sb("cmat_s", [128, 4 * 128], BF16)
    memTn = sb("memTn", [128, 8 * MEM], BF16)
    kmemT = sb("kmemT", [128, 2 * MEM], BF16)
    vmem = sb("vmem", [128, 2 * MEM], BF16)
    epsc = sb("epsc", [128, 2], F32)
    if phase == "A":
        xh_t = sb("xh_s", [128, NCH * 8], F32)
        hflag = sb("hflag_s", [128, 8], F32)
        uh_t = sb("uh", [128, 6 * 3 * 2], F32)
    else:
        lamt = sb("lamt", [128, 512], F32)
        lamv = sb("lamv", [128, 8], F32)
        kc_t = sb("kc_s", [128, 128], F32)
        qT_t = sb("qT", [128, 6 * T], BF16)
        kst = [sb(f"kst{i}", [128, 1024], BF16) for i in range(2)]
        vst = [sb(f"vst{i}", [128, 1024], BF16) for i in range(2)]
        osb = [sb(f"osb{i}", [128, T], F32) for i in range(2)]
    PS = [pst(f"ps{i}") for i in range(8)]

    ones_mean = cmat[:, 0:128]
    blockones = cmat[:, 128:256]
    ones_b = cmat[:, 256:384]
    perm_m = cmat[:, 384:512]

    def cc(name, off=0, n=1):
        b = CST[name] + off
        return cst[:, b:b + n]

    def Rv(off, nbytes, dt=BF16):
        ap = R_t[:, off // 2:(off + nbytes) // 2]
        if dt == F32:
            ap = ap.bitcast(F32)
        elif dt == I32:
            ap = ap.bitcast(I32)
        keys = [("R", g) for g in range(off // 1024, (off + nbytes + 1023) // 1024)]
        return ap, keys

    def h2_view(t, c, n=T):
        ap, k = Rv((t * 8 + c) * 1024, 1024)
        return ap[:, 0:n], k

    def h_view(buf, c, n=T):
        ap, k = Rv((buf * 8 + c) * 1024, 1024)
        return ap[:, 0:n], k

    SCR = 16384

    wstate = {"i": 0}

    def wload():
        i = wstate["i"]
        wstate["i"] += 1
        slot = i % NSL
        S.dma("pool", lambda h, i=i, slot=slot: h.dma_start(out=ring[slot][:], in_=wts[i]),
              W=[("w", slot)], stream=f"w{slot}")
        return slot

    def wcol(slot):
        return ring[slot][:].rearrange("p (k n) -> p k n", k=8)

    def wdown(slot):
        return ring[slot][:].rearrange("p (k n) -> p k n", k=4)

    pstate = {"i": 0}

    def bank():
        b = pstate["i"] % 8
        pstate["i"] += 1
        return b

    def mm(out_b, out_ap, lhsT, rhs, start, stop, R, extraW=()):
        S.op("pe", lambda h: h.matmul(out_ap, lhsT, rhs, start=start, stop=stop),
             R=R, W=[("ps", out_b)] + list(extraW))

    tog = {"sq": 0, "rs": 0, "sqf": 0, "pT": 0, "a": 0, "h": 0}

    def nxt(name, n):
        v = tog[name]
        tog[name] = (v + 1) % n
        return v

    def rstd_from(psb, n, eps_col, out_rs_i):
        rs = rs_t[out_rs_i]
        S.op("act", lambda h: h.activation(out=rs[:, :n], in_=PS[psb][:, :n], func=AF.Sqrt, bias=epsc[:, eps_col:eps_col + 1]),
             R=[("ps", psb), "epsc"], W=[("rs", out_rs_i)])
        S.op("dve", lambda h: h.reciprocal(rs[:, :n], rs[:, :n]), R=[("rs", out_rs_i)], W=[("rs", out_rs_i)])

    def norm_tile(xap, xkey, n, gname, hout):
        b = bank()
        for c in range(NCH):
            si = nxt("sq", 2)
            S.op("act", lambda h, c=c, si=si: h.activation(out=sq_t[si][:, :n], in_=xap(c), func=AF.Square),
                 R=[xkey(c)], W=[("sq", si)])
            mm(b, PS[b][:, :n], ones_mean, sq_t[si][:, :n], c == 0, c == NCH - 1, R=[("sq", si), "cmat"])
        ri = nxt("rs", 2)
        rstd_from(b, n, 0, ri)
        for c in range(NCH):
            hap, hk = hout(c)
            S.op("dve", lambda h, c=c, hap=hap: h.scalar_tensor_tensor(out=hap, in0=xap(c), scalar=cc(gname, c), in1=rs_t[ri][:, :n],
                                                                       op0=ALU.mult, op1=ALU.mult),
                 R=[xkey(c), ("rs", ri), "cst"], W=hk)

    def proj_chunk(b, n, slots, oc, hin, nk=NCH):
        slot = slots[oc // 4]
        co = (oc % 4) * 128
        for kc in range(nk):
            hap, hk = hin(kc)
            mm(b, PS[b][:, :n], wcol(slot)[:, kc, co:co + 128], hap, kc == 0, kc == nk - 1, R=hk + [("w", slot)])

    S.dma("sp", lambda h: h.dma_start(out=cst[:], in_=cst_d), W=["cst"], stream="c0")
    S.dma("pool", lambda h: h.dma_start(out=cmat[:], in_=cmat_d), W=["cmat"], stream="c1")
    x3 = x_t[:].rearrange("p (c n) -> p c n", c=NCH)
    xin3 = xin_d.rearrange("p (c n) -> p c n", c=NCH)
    for t in range(NT):
        S.dma("sp", lambda h, t=t: h.dma_start(out=x3[:, :, t * T:(t + 1) * T], in_=xin3[:, :, t * T:(t + 1) * T]),
              W=[("x", t, c) for c in range(NCH)], stream=f"x{t}")
    S.op("dve", lambda h: h.memset(epsc[:, 0:1], EPS), W=["epsc"])
    S.op("dve", lambda h: h.memset(epsc[:, 1:2], SUBLN_EPS), W=["epsc"])
    if phase == "A":
        S.dma("sp", lambda h: h.dma_start(out=xh_t[:], in_=xh_d), W=[("xh", c) for c in range(NCH)], stream="c2")
        S.dma("sp", lambda h: h.dma_start(out=hflag[:], in_=hflag_d), W=["hflag"], stream="c3")
    else:
        S.dma("sp", lambda h: h.dma_start(out=lamt[:], in_=lam_d), W=["lamt"], stream="c2")
        S.dma("sp", lambda h: h.dma_start(out=kc_t[:], in_=kc_d), W=["kc"], stream="c3")

    def xmain(t):
        return (lambda c: x_t[:, c * NTOK + t * T: c * NTOK + (t + 1) * T]), (lambda c: ("x", t, c))

    memf, memfk = Rv(SCR, 8192, F32)
    S.dma("sp", lambda h: h.dma_start(out=memf, in_=memT_d), W=memfk, stream="c4")
    norm_tile(lambda c: memf[:, c * MEM:(c + 1) * MEM], lambda c: memfk[c], MEM, "mem_norm",
              lambda c: (memTn[:, c * MEM:(c + 1) * MEM], ["memTn"]))

    def mem_kv(l, slot):
        w = wcol(slot)
        for c2 in range(2):
            b = bank()
            for kc in range(NCH):
                mm(b, PS[b][:, :MEM], w[:, kc, c2 * 128:(c2 + 1) * 128], memTn[:, kc * MEM:(kc + 1) * MEM], kc == 0, kc == NCH - 1,
                   R=["memTn", ("w", slot)])
            si = nxt("sq", 2)
            S.op("act", lambda h, b=b, si=si: h.activation(out=sq_t[si][:, :MEM], in_=PS[b][:, :MEM], func=AF.Square),
                 R=[("ps", b)], W=[("sq", si)])
            b2 = bank()
            mm(b2, PS[b2][:, :MEM], blockones, sq_t[si][:, :MEM], True, True, R=[("sq", si), "cmat"])
            ri = nxt("rs", 2)
            rstd_from(b2, MEM, 0, ri)
            S.op("dve", lambda h, b=b, c2=c2, ri=ri: h.scalar_tensor_tensor(out=kmemT[:, c2 * MEM:(c2 + 1) * MEM], in0=PS[b][:, :MEM],
                                                                            scalar=cc(("mem_k", l)), in1=rs_t[ri][:, :MEM],
                                                                            op0=ALU.mult, op1=ALU.mult),
                 R=[("ps", b), ("rs", ri), "cst"], W=["kmemT"])
        for mc in range(2):
            b = bank()
            for kc in range(NCH):
                mm(b, PS[b][:, :256], memTn[:, kc * MEM + mc * 128: kc * MEM + (mc + 1) * 128], w[:, kc, 256:512], kc == 0, kc == NCH - 1,
                   R=["memTn", ("w", slot)])
            S.op("act", lambda h, b=b, mc=mc: h.copy(out=vmem[:, mc * 256:(mc + 1) * 256], in_=PS[b][:, :256]),
                 R=[("ps", b)], W=["vmem"])

    def mem_attn(l, n, qproj, qn_view):
        for c2 in range(2):
            b = bank()
            qproj(c2, b)
            si = nxt("sq", 2)
            S.op("act", lambda h, b=b, si=si: h.activation(out=sq_t[si][:, :n], in_=PS[b][:, :n], func=AF.Square),
                 R=[("ps", b)], W=[("sq", si)])
            b2 = bank()
            mm(b2, PS[b2][:, :n], blockones, sq_t[si][:, :n], True, True, R=[("sq", si), "cmat"])
            ri = nxt("rs", 2)
            rstd_from(b2, n, 0, ri)
            qn, qnk = qn_view(c2)
            S.op("dve", lambda h, b=b, ri=ri, qn=qn: h.scalar_tensor_tensor(out=qn, in0=PS[b][:, :n], scalar=cc(("mem_q", l)),
                                                                            in1=rs_t[ri][:, :n], op0=ALU.mult, op1=ALU.mult),
                 R=[("ps", b), ("rs", ri), "cst"], W=qnk)
            bo = bank()
            bd = bank()
            for hh in range(2):
                r0 = 64 * hh
                for mc in range(2):
                    bs = bank()
                    mm(bs, PS[bs][:, :n], kmemT[r0:r0 + 64, c2 * MEM + mc * 128: c2 * MEM + (mc + 1) * 128], qn[r0:r0 + 64, :],
                       True, True, R=qnk + ["kmemT"])
                    pi = nxt("pT", 4)
                    S.op("act", lambda h, bs=bs, pi=pi: h.activation(out=pT_t[pi][:, :n], in_=PS[bs][:, :n], func=AF.Exp, scale=0.125),
                         R=[("ps", bs)], W=[("pT", pi)])
                    hcol = (2 * c2 + hh) * 64
                    mm(bo, PS[bo][r0:r0 + 64, :n], vmem[:, mc * 256 + hcol: mc * 256 + hcol + 64], pT_t[pi][:, :n], mc == 0, mc == 1,
                       R=[("pT", pi), "vmem"])
                    mm(bd, PS[bd][r0:r0 + 64, :n], ones_b[:, 0:64], pT_t[pi][:, :n], mc == 0, mc == 1, R=[("pT", pi), "cmat"])
            S.op("dve", lambda h, bd=bd: h.reciprocal(rd_t[:, :n], PS[bd][:, :n]), R=[("ps", bd)], W=["rd"])
            S.op("dve", lambda h, bo=bo, c2=c2: h.tensor_tensor(out=mm_t[:, (6 + c2) * T:(6 + c2) * T + n], in0=PS[bo][:, :n], in1=rd_t[:, :n],
                                                                op=ALU.mult),
                 R=[("ps", bo), "rd"], W=[("mm", 6 + c2)])

    def wo_apply(n, slots, xap, xkey):
        for oc in range(NCH):
            b = bank()
            proj_chunk(b, n, slots, oc, lambda kc: (mm_t[:, kc * T: kc * T + n], [("mm", kc)]))
            S.op("dve", lambda h, b=b, oc=oc: h.tensor_tensor(out=xap(oc), in0=xap(oc), in1=PS[b][:, :n], op=ALU.add),
                 R=[("ps", b), xkey(oc)], W=[xkey(oc)])

    def mlp(l, tiles):
        for (n, xap, xkey, ti) in tiles:
            norm_tile(xap, xkey, n, ("norm_mlp", l), lambda c, ti=ti, n=n: h2_view(ti, c, n))
        for g in range(8):
            su = wload()
            sd = wload()
            for (n, xap, xkey, ti) in tiles:
                ai = nxt("a", 2)
                for oc in range(4):
                    b = bank()
                    for kc in range(NCH):
                        hap, hk = h2_view(ti, kc, n)
                        mm(b, PS[b][:, :n], wcol(su)[:, kc, oc * 128:(oc + 1) * 128], hap, kc == 0, kc == NCH - 1, R=hk + [("w", su)])
                    fi = nxt("sqf", 2)
                    S.op("act", lambda h, b=b, fi=fi, n=n: h.activation(out=sqf_t[fi][:, :n], in_=PS[b][:, :n], func=AF.Square),
                         R=[("ps", b)], W=[("sqf", fi)])
                    S.op("dve", lambda h, b=b, fi=fi, ai=ai, oc=oc, n=n: h.scalar_tensor_tensor(
                        out=a_t[ai][:, oc * T: oc * T + n], in0=PS[b][:, :n], scalar=0.0, in1=sqf_t[fi][:, :n], op0=ALU.is_gt, op1=ALU.mult),
                        R=[("ps", b), ("sqf", fi)], W=[("a", ai, oc)])
                for oc in range(NCH):
                    b = bank()
                    for kc in range(4):
                        mm(b, PS[b][:, :n], wdown(sd)[:, kc, oc * 128:(oc + 1) * 128], a_t[ai][:, kc * T: kc * T + n], kc == 0, kc == 3,
                           R=[("a", ai, kc), ("w", sd)])
                    S.op("dve", lambda h, b=b, oc=oc, xap=xap, n=n: h.tensor_tensor(out=xap(oc), in0=xap(oc), in1=PS[b][:, :n], op=ALU.add),
                         R=[("ps", b), xkey(oc)], W=[xkey(oc)])

    def a_layer(l):
        smk = wload()
        mem_kv(l, smk)
        win = [wload() for _ in range(5)]
        wo = [wload() for _ in range(2)]
        tiles = [(8, (lambda c: xh_t[:, c * 8:(c + 1) * 8]), (lambda c: ("xh", c)), "h")]
        for t in range(NT):
            xa, xk = xmain(t)
            tiles.append((T, xa, xk, t))
        if DEBUG is not None:
            tiles = [tt for tt in tiles if tt[3] in DEBUG["tiles"]]
        for (n, xap, xkey, ti) in tiles:
            hb = nxt("h", 2)
            norm_tile(xap, xkey, n, ("norm_mix", l), lambda c, hb=hb, n=n: h_view(hb, c, n))
            if DEBUG is not None and ti == 0 and l == 0:
                for c in range(NCH):
                    hap, hk = h_view(hb, c, n)
                    S.dma("sp", lambda h, c=c, hap=hap: h.dma_start(out=dbg_h[:, c * T:(c + 1) * T], in_=hap), R=hk, stream=f"dbg{c}")
            hin = lambda kc, hb=hb, n=n: h_view(hb, kc, n)
            for j in range(6):
                par = j % 2
                hvs, hvk = Rv(SCR + par * 2048, 2048, F32)
                u, uk = Rv(SCR + 4096 + par * 2560, 2560, F32)
                y, yk = Rv(SCR + 9216 + par * 2048, 2048, F32)
                bh = bank()
                proj_chunk(bh, n, win, 12 + j, hin)
                S.op("act", lambda h, bh=bh, hvs=hvs, n=n: h.copy(out=hvs[:, :n], in_=PS[bh][:, :n]), R=[("ps", bh)], W=hvk)
                bc = bank()
                proj_chunk(bc, n, win, 6 + j, hin)
                if ti == "h":
                    S.op("dve", lambda h, u=u: h.memset(u[:, 0:2], 0.0), W=uk)
                else:
                    kind = 1 if ti == 0 else (2 if ti == 2 else 0)
                    S.op("dve", lambda h, u=u, j=j, kind=kind: h.tensor_copy(u[:, 0:2], uh_t[:, (j * 3 + kind) * 2:(j * 3 + kind) * 2 + 2]),
                         R=[("uh", j)], W=uk)
                S.op("dve", lambda h, bc=bc, u=u, hvs=hvs, n=n: h.tensor_tensor(out=u[:, 2:2 + n], in0=PS[bc][:, :n], in1=hvs[:, :n], op=ALU.mult),
                     R=[("ps", bc)] + hvk, W=uk)
                if ti == "h":
                    S.op("dve", lambda h, u=u: h.tensor_tensor(out=u[:, 2:10], in0=u[:, 2:10], in1=hflag[:, 0:8], op=ALU.mult),
                         R=uk + ["hflag"], W=uk)
                    S.op("dve", lambda h, u=u, j=j: h.tensor_copy(uh_t[:, (j * 3 + 1) * 2:(j * 3 + 1) * 2 + 2], u[:, 4:6]), R=uk, W=[("uh", j)])
                    S.op("dve", lambda h, u=u, j=j: h.tensor_copy(uh_t[:, (j * 3 + 2) * 2:(j * 3 + 2) * 2 + 2], u[:, 8:10]), R=uk, W=[("uh", j)])
                else:
                    S.op("dve", lambda h, u=u, j=j, n=n: h.tensor_copy(uh_t[:, (j * 3) * 2:(j * 3) * 2 + 2], u[:, n:n + 2]), R=uk, W=[("uh", j)])
                S.op("act", lambda h, u=u, y=y, j=j, n=n: h.activation(out=y[:, :n], in_=u[:, 2:2 + n], func=AF.Copy, scale=cc(("conv", l, 2), j)),
                     R=uk + ["cst"], W=yk)
                S.op("dve", lambda h, u=u, y=y, j=j, n=n: h.scalar_tensor_tensor(out=y[:, :n], in0=u[:, 1:1 + n], scalar=cc(("conv", l, 1), j),
                                                                               in1=y[:, :n], op0=ALU.mult, op1=ALU.add),
                     R=uk + yk + ["cst"], W=yk)
                S.op("dve", lambda h, u=u, y=y, j=j, n=n: h.scalar_tensor_tensor(out=y[:, :n], in0=u[:, 0:n], scalar=cc(("conv", l, 0), j),
                                                                               in1=y[:, :n], op0=ALU.mult, op1=ALU.add),
                     R=uk + yk + ["cst"], W=yk)
                bg = bank()
                proj_chunk(bg, n, win, j, hin)
                S.op("dve", lambda h, bg=bg, y=y, j=j, n=n: h.tensor_tensor(out=mm_t[:, j * T: j * T + n], in0=PS[bg][:, :n], in1=y[:, :n], op=ALU.mult),
                     R=[("ps", bg)] + yk, W=[("mm", j)])
            mem_attn(l, n, lambda c2, bq, n=n, hin=hin: proj_chunk(bq, n, win, 18 + c2, hin),
                     lambda c2, n=n: (lambda v: (v[0][:, :n], v[1]))(Rv(SCR + 13312 + c2 * 1024, 1024)))
            if DEBUG is not None and ti == 0 and l == 0:
                S.dma("sp", lambda h: h.dma_start(out=dbg_mm, in_=mm_t[:]), R=[("mm", c) for c in range(NCH)], stream="dbgmm")
            wo_apply(n, wo, xap, xkey)
            if DEBUG is not None and ti == 0 and l == 0:
                S.dma("sp", lambda h: h.dma_start(out=dbg_x1.rearrange("p (c n) -> p c n", c=NCH), in_=x3[:, :, 0:T]), R=[("x", 0, c) for c in range(NCH)], stream="dbgx1")
        mlp(l, tiles_for_mlp(tiles))

    def tiles_for_mlp(tiles):
        out = []
        for (n, xap, xkey, ti) in tiles:
            out.append((n, xap, xkey, 4 if ti == "h" else ti))
        return out

    h2h_t = sb("h2h", [128, NCH * 8], BF16)
    _h2_view_orig = h2_view

    def h2_view(t, c, n=T):
        if t == 4:
            return h2h_t[:, c * 8: c * 8 + n], [("h2h", c)]
        return _h2_view_orig(t, c, n)

    def rope_tables(t, Cap, Ck, Sap, Sk, tmp, tmpk, ki, kik, posb, posk):
        S.dma("sp", lambda h: h.dma_start(out=posb, in_=pos_d[:, t * T:(t + 1) * T]), W=posk, stream="pos")
        TWO_PI = 2.0 * np.pi
        C1 = 6.28125
        C2 = TWO_PI - C1
        for which, (oap, ok) in enumerate(((Sap, Sk), (Cap, Ck))):
            if which == 0:
                S.op("dve", lambda h, oap=oap: h.tensor_scalar(out=oap, in0=posb, scalar1=cc("invf"), scalar2=None, op0=ALU.mult),
                     R=posk + ["cst"], W=ok)
            else:
                S.op("dve", lambda h, oap=oap: h.tensor_scalar(out=oap, in0=posb, scalar1=cc("invf"), scalar2=float(np.pi / 2), op0=ALU.mult, op1=ALU.add),
                     R=posk + ["cst"], W=ok)
            S.op("dve", lambda h, oap=oap: h.tensor_scalar(out=tmp, in0=oap, scalar1=float(1.0 / TWO_PI), scalar2=None, op0=ALU.mult),
                 R=ok, W=tmpk)
            S.op("dve", lambda h: h.tensor_copy(ki, tmp), R=tmpk, W=kik)
            S.op("dve", lambda h: h.tensor_copy(tmp, ki), R=kik, W=tmpk)
            S.op("dve", lambda h, oap=oap: h.scalar_tensor_tensor(out=oap, in0=tmp, scalar=-C1, in1=oap, op0=ALU.mult, op1=ALU.add),
                 R=tmpk + ok, W=ok)
            S.op("dve", lambda h, oap=oap: h.scalar_tensor_tensor(out=oap, in0=tmp, scalar=-C2, in1=oap, op0=ALU.mult, op1=ALU.add),
                 R=tmpk + ok, W=ok)
            S.op("dve", lambda h, oap=oap: h.tensor_scalar(out=oap, in0=oap, scalar1=3.1415925, scalar2=-3.1415925, op0=ALU.min, op1=ALU.max),
                 R=ok, W=ok)
            S.op("act", lambda h, oap=oap: h.activation(out=oap, in_=oap, func=AF.Sin), R=ok, W=ok)
        S.op("dve", lambda h: h.tensor_scalar(out=Sap, in0=Sap, scalar1=cc("sgn"), scalar2=None, op0=ALU.mult), R=Sk + ["cst"], W=Sk)

    def head_norm_rope(b, n, gname, Cap, Ck, Sap, Sk, out_ap, out_k, tq, tqk, tb, tbk):
        si = nxt("sq", 2)
        S.op("act", lambda h: h.activation(out=sq_t[si][:, :n], in_=PS[b][:, :n], func=AF.Square), R=[("ps", b)], W=[("sq", si)])
        b2 = bank()
        mm(b2, PS[b2][:, :n], blockones, sq_t[si][:, :n], True, True, R=[("sq", si), "cmat"])
        ri = nxt("rs", 2)
        rstd_from(b2, n, 0, ri)
        S.op("dve", lambda h: h.scalar_tensor_tensor(out=tq, in0=PS[b][:, :n], scalar=cc(gname), in1=rs_t[ri][:, :n], op0=ALU.mult, op1=ALU.mult),
             R=[("ps", b), ("rs", ri), "cst"], W=tqk)
        S.op("act", lambda h: h.copy(out=tb, in_=tq), R=tqk, W=tbk)
        b3 = bank()
        mm(b3, PS[b3][:, :n], perm_m, tb, True, True, R=tbk + ["cmat"])
        S.op("dve", lambda h: h.tensor_tensor(out=tq, in0=tq, in1=Cap, op=ALU.mult), R=tqk + Ck, W=tqk)
        fi = nxt("sqf", 2)
        S.op("dve", lambda h: h.tensor_tensor(out=sqf_t[fi][:, :n], in0=PS[b3][:, :n], in1=Sap, op=ALU.mult), R=[("ps", b3)] + Sk, W=[("sqf", fi)])
        S.op("dve", lambda h: h.tensor_tensor(out=out_ap, in0=tq, in1=sqf_t[fi][:, :n], op=ALU.add), R=tqk + [("sqf", fi)], W=out_k)

    def kv_stage():
        wk = [wload() for _ in range(3)]
        Cap, Ck = Rv(SCR, 2048, F32)
        Sap, Sk = Rv(SCR + 2048, 2048, F32)
        tmp, tmpk = Rv(SCR + 4096, 2048, F32)
        ki, kik = Rv(SCR + 6144, 2048, I32)
        posb, posk = Rv(SCR + 8192, 2048, F32)
        tq, tqk = Rv(SCR + 10240, 2048, F32)
        tb, tbk = Rv(SCR + 12288, 1024, BF16)
        for t in range(NT):
            xa, xk = xmain(t)
            hb = nxt("h", 2)
            norm_tile(xa, xk, T, "kv_norm", lambda c, hb=hb: h_view(hb, c, T))
            hin = lambda kc, hb=hb: h_view(hb, kc, T)
            rope_tables(t, Cap, Ck, Sap, Sk, tmp, tmpk, ki, kik, posb, posk)
            for hd in range(6):
                b = bank()
                proj_chunk(b, T, wk, hd, hin)
                ko, kok = Rv(SCR + 13312 + (hd % 2) * 1024, 1024, BF16)
                head_norm_rope(b, T, "k_norm", Cap, Ck, Sap, Sk, ko, kok, tq, tqk, tb, tbk)
                S.dma("sp", lambda h, hd=hd, ko=ko, t=t: h.dma_start(out=kT_d[hd, :, t * T:(t + 1) * T], in_=ko), R=kok, stream=f"ko{hd % 2}")
            for s4 in range(4):
                vo = a_t[s4 % 2]
                for (c0, cn, wsl, wc0) in ((0, 256, wk[1], 256), (256, 512, wk[2], 0)):
                    b = bank()
                    for kc in range(NCH):
                        hap, hk = hin(kc)
                        mm(b, PS[b][:, :cn], hap[:, s4 * 128:(s4 + 1) * 128], wcol(wsl)[:, kc, wc0:wc0 + cn], kc == 0, kc == NCH - 1,
                           R=hk + [("w", wsl)])
                    S.op("act", lambda h, b=b, vo=vo, c0=c0, cn=cn: h.copy(out=vo[:, c0:c0 + cn], in_=PS[b][:, :cn]),
                         R=[("ps", b)], W=[("a", s4 % 2, 0), ("a", s4 % 2, 1)])
                S.dma("sp", lambda h, vo=vo, t=t, s4=s4: h.dma_start(out=v_d[t * T + s4 * 128: t * T + (s4 + 1) * 128, :], in_=vo[:, 0:768]),
                      R=[("a", s4 % 2, 0), ("a", s4 % 2, 1)], stream=f"vo{s4 % 2}")

    def b_layer(l):
        j = l - 2
        smk = wload()
        mem_kv(l, smk)
        wq = [wload() for _ in range(2)]
        wo = [wload() for _ in range(2)]
        lam_init = 0.8 - 0.6 * float(np.exp(-0.3 * l))
        lt = lamt[:, j * 256:(j + 1) * 256]
        S.op("dve", lambda h: h.tensor_tensor(out=sqf_t[0][:, 0:64], in0=lt[:, 0:64], in1=lt[:, 64:128], op=ALU.mult), R=["lamt"], W=[("sqf", 0)])
        S.op("dve", lambda h: h.tensor_tensor(out=sqf_t[0][:, 64:128], in0=lt[:, 128:192], in1=lt[:, 192:256], op=ALU.mult), R=["lamt", ("sqf", 0)], W=[("sqf", 0)])
        S.op("dve", lambda h: h.tensor_reduce(out=lamv[:, 2:4], in_=sqf_t[0][:, 0:128].rearrange("p (a b) -> p a b", a=2), axis=mybir.AxisListType.X, op=ALU.add),
             R=[("sqf", 0)], W=["lamv"])
        S.op("act", lambda h: h.activation(out=lamv[:, 4:6], in_=lamv[:, 2:4], func=AF.Exp), R=["lamv"], W=["lamv"])
        S.op("dve", lambda h: h.tensor_tensor(out=lamv[:, 0:1], in0=lamv[:, 4:5], in1=lamv[:, 5:6], op=ALU.subtract), R=["lamv"], W=["lamv"])
        S.op("dve", lambda h: h.tensor_scalar(out=lamv[:, 1:2], in0=lamv[:, 0:1], scalar1=lam_init, scalar2=-1.0, op0=ALU.add, op1=ALU.mult), R=["lamv"], W=["lamv"])

        Cap, Ck = Rv(SCR, 2048, F32)
        Sap, Sk = Rv(SCR + 2048, 2048, F32)
        tmp, tmpk = Rv(SCR + 4096, 2048, F32)
        ki, kik = Rv(SCR + 6144, 2048, I32)
        posb, posk = Rv(SCR + 8192, 2048, F32)
        tq, tqk = Rv(SCR + 10240, 2048, F32)
        tb, tbk = Rv(SCR + 12288, 1024, BF16)
        qcb, qck = Rv(SCR + 13312, 2048, F32)
        tiles = []
        for t in range(NT):
            if DEBUG is not None and t not in DEBUG["tiles"]:
                continue
            xa, xk = xmain(t)
            tiles.append((T, xa, xk, t))
            hb = nxt("h", 2)
            norm_tile(xa, xk, T, ("norm_mix", l), lambda c, hb=hb: h_view(hb, c, T))
            hin = lambda kc, hb=hb: h_view(hb, kc, T)
            rope_tables(t, Cap, Ck, Sap, Sk, tmp, tmpk, ki, kik, posb, posk)
            S.dma("sp", lambda h, t=t: h.dma_start(out=qcb, in_=qc_d[:, t * T:(t + 1) * T]), W=qck, stream="qc")
            for hd in range(6):
                b = bank()
                proj_chunk(b, T, wq, hd, hin)
                head_norm_rope(b, T, ("b_q", j), Cap, Ck, Sap, Sk, qT_t[:, hd * T:(hd + 1) * T], [("qT", hd)], tq, tqk, tb, tbk)
            mem_attn(l, T, lambda c2, bq, hin=hin: proj_chunk(bq, T, wq, 6 + c2, hin),
                     lambda c2: Rv(SCR + 8192 + c2 * 1024, 1024))
            if DEBUG is not None:
                S.dma("sp", lambda h: h.dma_start(out=dbg_q, in_=qT_t[:]), R=[("qT", hd) for hd in range(6)], stream="dbgq")
            diff_attn(l, j, t, qcb, qck, lam_init)
            if DEBUG is not None:
                S.dma("sp", lambda h: h.dma_start(out=dbg_mm, in_=mm_t[:]), R=[("mm", c) for c in range(NCH)], stream="dbgmm")
            wo_apply(T, wo, xa, xk)
            if DEBUG is not None:
                S.dma("sp", lambda h, t=t: h.dma_start(out=dbg_x1.rearrange("p (c n) -> p c n", c=NCH), in_=x3[:, :, t * T:(t + 1) * T]), R=[("x", t, c) for c in range(NCH)], stream="dbgx1")
        mlp(l, tiles)

    def diff_attn(l, j, t, qcb, qck, lam_init):
        NSB = 8 if t < 2 else 16
        for hd in range(6):
            bO1, bO2, bD1, bD2 = 4, 5, 6, 7
            first = True
            for sbk in range(NSB):
                r = sbk if sbk < 8 else 15 - sbk
                half = 0 if sbk < 8 else 1
                ks = (hd * NSB + sbk) % 2
                S.dma("sp", lambda h, r=r, half=half, ks=ks, hd=hd: h.dma_start(out=kst[ks][:], in_=kTall_d[r, hd, :, half * BLK:(half + 1) * BLK]),
                      W=[("kst", ks)], stream=f"kst{ks}")
                S.dma("sp", lambda h, r=r, half=half, ks=ks, hd=hd: h.dma_start(
                    out=vst[ks][:].rearrange("p (b e) -> p b e", b=8),
                    in_=vall_d[r, half * BLK:(half + 1) * BLK, hd * 128:(hd + 1) * 128].rearrange("(b p) e -> p b e", p=128)),
                    W=[("vst", ks)], stream=f"vst{ks}")
                for kb in range(8):
                    kcol = sbk * 8 + kb
                    bS = [(2 * kb) % 4, (2 * kb + 1) % 4]
                    pis = []
                    for m in range(2):
                        r0 = 64 * m
                        mm(bS[m], PS[bS[m]][:, :T], kst[ks][r0:r0 + 64, kb * 128:(kb + 1) * 128], qT_t[r0:r0 + 64, hd * T:(hd + 1) * T],
                           True, True, R=[("kst", ks), ("qT", hd)])
                    for m in range(2):
                        pi = nxt("pT", 4)
                        pis.append(pi)
                        S.op("act", lambda h, m=m, pi=pi, bS=bS: h.activation(out=pT_t[pi][:], in_=PS[bS[m]][:, :T], func=AF.Exp, scale=0.125),
                             R=[("ps", bS[m])], W=[("pT", pi)])
                        S.op("dve", lambda h, pi=pi, kcol=kcol: h.scalar_tensor_tensor(out=pT_t[pi][:], in0=qcb, scalar=kc_t[:, kcol:kcol + 1],
                                                                                      in1=pT_t[pi][:], op0=ALU.is_ge, op1=ALU.mult),
                             R=qck + ["kc", ("pT", pi)], W=[("pT", pi)])
                    last = (sbk == NSB - 1 and kb == 7)
                    for m, (bo, bd) in enumerate(((bO1, bD1), (bO2, bD2))):
                        mm(bo, PS[bo][:, :T], vst[ks][:, kb * 128:(kb + 1) * 128], pT_t[pis[m]][:], first, last, R=[("pT", pis[m]), ("vst", ks)])
                        mm(bd, PS[bd][:, :T], ones_b, pT_t[pis[m]][:], first, last, R=[("pT", pis[m]), "cmat"])
                    first = False
            o1, o2 = osb[0], osb[1]
            S.op("dve", lambda h: h.reciprocal(rd_t[:], PS[bD1][:, :T]), R=[("ps", bD1)], W=["rd"])
            S.op("dve", lambda h: h.tensor_tensor(out=o1[:], in0=PS[bO1][:, :T], in1=rd_t[:], op=ALU.mult), R=[("ps", bO1), "rd"], W=[("osb", 0)])
            S.op("dve", lambda h: h.reciprocal(rd_t[:], PS[bD2][:, :T]), R=[("ps", bD2), ("osb", 0)], W=["rd"])
            S.op("dve", lambda h: h.tensor_tensor(out=o2[:], in0=PS[bO2][:, :T], in1=rd_t[:], op=ALU.mult), R=[("ps", bO2), "rd"], W=[("osb", 1)])
            S.op("dve", lambda h: h.scalar_tensor_tensor(out=o1[:], in0=o2[:], scalar=lamv[:, 1:2], in1=o1[:], op0=ALU.mult, op1=ALU.add),
                 R=[("osb", 0), ("osb", 1), "lamv"], W=[("osb", 0)])
            si = nxt("sq", 2)
            S.op("act", lambda h, si=si: h.activation(out=sq_t[si][:], in_=o1[:], func=AF.Square), R=[("osb", 0)], W=[("sq", si)])
            b2 = bank() % 4
            mm(b2, PS[b2][:, :T], ones_mean, sq_t[si][:], True, True, R=[("sq", si), "cmat"])
            ri = nxt("rs", 2)
            S.op("act", lambda h, b2=b2, ri=ri: h.activation(out=rs_t[ri][:], in_=PS[b2][:, :T], func=AF.Sqrt, bias=epsc[:, 1:2], scale=8.0),
                 R=[("ps", b2), "epsc"], W=[("rs", ri)])
            S.op("dve", lambda h, ri=ri: h.reciprocal(rs_t[ri][:], rs_t[ri][:]), R=[("rs", ri)], W=[("rs", ri)])
            S.op("dve", lambda h, ri=ri: h.scalar_tensor_tensor(out=o1[:], in0=o1[:], scalar=cc(("subln", j)), in1=rs_t[ri][:], op0=ALU.mult, op1=ALU.mult),
                 R=[("osb", 0), ("rs", ri), "cst"], W=[("osb", 0)])
            S.op("act", lambda h, hd=hd: h.activation(out=mm_t[:, hd * T:(hd + 1) * T], in_=o1[:], func=AF.Copy, scale=float(1.0 - lam_init)),
                 R=[("osb", 0)], W=[("mm", hd)])

    if phase == "A" and DEBUG is not None:
        dbg_h = dram("dbg_h", [128, NCH * T], BF16, "ExternalOutput")
        dbg_mm = dram("dbg_mm", [128, NCH * T], BF16, "ExternalOutput")
        dbg_x1 = dram("dbg_x1", [128, NCH * T], F32, "ExternalOutput")
        for l in DEBUG["layers"]:
            a_layer(l)
    elif phase == "A":
        a_layer(0)
        a_layer(1)
        kv_stage()
    elif DEBUG is not None:
        dbg_q = dram("dbg_q", [128, 6 * T], BF16, "ExternalOutput")
        dbg_mm = dram("dbg_mm", [128, NCH * T], BF16, "ExternalOutput")
        dbg_x1 = dram("dbg_x1", [128, NCH * T], F32, "ExternalOutput")
        for l in DEBUG["layers"]:
            b_layer(l)
    else:
        b_layer(2)
        b_layer(3)
    xout3 = xout_d.rearrange("p (c n) -> p c n", c=NCH)
    for t in range(NT):
        S.dma("sp", lambda h, t=t: h.dma_start(out=xout3[:, :, t * T:(t + 1) * T], in_=x3[:, :, t * T:(t + 1) * T]),
              R=[("x", t, c) for c in range(NCH)], stream="xo")
    outs = [o for o in S.ops["sp"] if o["stream"] is not None and (o["stream"].startswith("xo") or o["stream"].startswith("ko") or o["stream"].startswith("vo"))]
    last_by_stream = {}
    for o in outs:
        last_by_stream[o["stream"]] = o
    fin = S.op("sp", lambda h: h.nop(), R=(), W=())
    for o in last_by_stream.values():
        fin["deps"][o["id"]] = o

    S.finalize(None)
    sems = {}
    for e in S.order:
        sems[("e", e)] = es.enter_context(nc.semaphore(f"sem_{e}"))
    for sname in S.streams:
        sems[("s", sname)] = es.enter_context(nc.semaphore(f"ds_{sname}"))
    block = es.enter_context(nc.Block())
    S.emit(block, sems)
    es.close()
    return nc


def _pk(w):
    K, N = w.shape
    return np.ascontiguousarray(w.reshape(K // 128, 128, N).transpose(1, 0, 2))


def _halfslabs_cols(w):
    out = []
    for c0 in range(0, w.shape[1], 512):
        out.append(_pk(w[:, c0:c0 + 512]).reshape(128, HS))
    return out


def _halfslab_rows(w):
    return _pk(w).reshape(128, HS)


def _mlp_slabs(w_up, w_down):
    out = []
    for g in range(8):
        out.append(_pk(w_up[:, g * 512:(g + 1) * 512]).reshape(128, HS))
        out.append(_halfslab_rows(w_down[g * 512:(g + 1) * 512, :]))
    return out


def _col128(v):
    return np.ascontiguousarray(v.reshape(-1, 128).T)


def _build_cst(inp):
    cst = np.zeros((128, NCST), np.float32)
    for l in range(4):
        cst[:, CST[("norm_mix", l)]:CST[("norm_mix", l)] + 8] = _col128(inp["norm_mix"][l])
        cst[:, CST[("norm_mlp", l)]:CST[("norm_mlp", l)] + 8] = _col128(inp["norm_mlp"][l])
        cst[:, CST[("mem_q", l)]] = np.tile(inp["mem_q_norm"][l], 2)
        cst[:, CST[("mem_k", l)]] = np.tile(inp["mem_k_norm"][l], 2)
    cst[:, CST["kv_norm"]:CST["kv_norm"] + 8] = _col128(inp["kv_norm"])
    cst[:, CST["mem_norm"]:CST["mem_norm"] + 8] = _col128(inp["mem_norm"])
    for l in range(2):
        for k in range(3):
            cst[:, CST[("conv", l, k)]:CST[("conv", l, k)] + 6] = _col128(inp["a_conv"][l, k])
    for j in range(2):
        cst[:, CST[("b_q", j)]] = np.tile(inp["b_q_norm"][j], 2)
        cst[:, CST[("subln", j)]] = inp["b_subln"][j]
    cst[:, CST["k_norm"]] = np.tile(inp["k_norm"], 2)
    d = np.arange(128) % 64
    invf = np.where(d < 16, 1.0 / (np.float32(THETA) ** (np.arange(0, 16, 2, dtype=np.float32) / np.float32(16)))[d % 8], 0.0)
    cst[:, CST["invf"]] = invf.astype(np.float32)
    cst[:, CST["sgn"]] = np.where(d < 8, -1.0, np.where(d < 16, 1.0, 0.0))
    return cst


def _build_cmat():
    cm = np.zeros((128, 4 * 128), np.float32)
    cm[:, 0:128] = 1.0 / 1024.0
    for b in range(2):
        cm[b * 64:(b + 1) * 64, 128 + b * 64:128 + (b + 1) * 64] = 1.0 / 64.0
    cm[:, 256:384] = 1.0
    for m in range(128):
        d = m % 64
        if d < 8:
            cm[m + 8, 384 + m] = 1.0
        elif d < 16:
            cm[m - 8, 384 + m] = 1.0
    return cm


def _core_tokens(c):
    a = np.arange(c * BLK, (c + 1) * BLK)
    b = np.arange((15 - c) * BLK, (16 - c) * BLK)
    return np.concatenate([a, b])


def _qk_perm():
    idx = []
    for h in range(6):
        idx += list(range(h * 64, (h + 1) * 64))
        idx += list(range(384 + h * 64, 384 + (h + 1) * 64))
    return np.array(idx)


_PROGS = {}


def _prog(phase):
    if phase not in _PROGS:
        _PROGS[phase] = build_program(phase)
    return _PROGS[phase]


def _featmajor(xT_core):
    n = xT_core.shape[1]
    return np.ascontiguousarray(xT_core.reshape(8, 128, n).transpose(1, 0, 2).reshape(128, 8 * n))


def _unfeat(a, n):
    return a.reshape(128, 8, n).transpose(1, 0, 2).reshape(1024, n)


def _run_A(inp):
    x = inp["x"][0]
    xT = np.ascontiguousarray(x.T)
    memT = _featmajor(np.ascontiguousarray(inp["mem"][0].T))
    cst = _build_cst(inp)
    cmat = _build_cmat()
    perm = _qk_perm()

    wA = []
    for l in range(2):
        wA += _halfslabs_cols(inp["w_mem_kv"][l])
        wA += _halfslabs_cols(inp["a_w_in"][l])
        wA += _halfslabs_cols(inp["w_o"][l])
        wA += _mlp_slabs(inp["w_up"][l], inp["w_down"][l])
    wkv = inp["w_kv"]
    wkv_p = np.concatenate([wkv[:, :768][:, perm], wkv[:, 768:]], axis=1)
    wA += _halfslabs_cols(wkv_p)
    wA = np.stack(wA)
    in_maps = []
    toks = [_core_tokens(c) for c in range(NCORES)]
    for c in range(NCORES):
        tk = toks[c]
        xh = np.zeros((1024, 8), np.float32)
        hf = np.ones((128, 8), np.float32)
        for bi, b0 in enumerate((c * BLK, (15 - c) * BLK)):
            if b0 >= 4:
                xh[:, bi * 4:(bi + 1) * 4] = xT[:, b0 - 4:b0]
            else:
                hf[:, bi * 4:(bi + 1) * 4] = 0.0
        in_maps.append({
            "wts": wA, "cst": cst, "memT": memT, "cmat": cmat,
            "xin": _featmajor(xT[:, tk]), "pos": np.ascontiguousarray(np.broadcast_to(tk.astype(np.float32)[None, :], (128, NTOK))),
            "xh": _featmajor(xh), "hflag": hf,
        })
    resA = run_bass_kernel_spmd(_prog("A"), in_maps, core_ids=list(range(NCORES)))
    return resA.results


def _run_B(inp, RA, cores=None):
    memT = _featmajor(np.ascontiguousarray(inp["mem"][0].T))
    cst = _build_cst(inp)
    cmat = _build_cmat()
    perm = _qk_perm()
    toks = [_core_tokens(c) for c in range(NCORES)]
    wB = []
    for l in range(2, 4):
        j = l - 2
        wB += _halfslabs_cols(inp["w_mem_kv"][l])
        wq = inp["b_w_q"][j]
        wq_p = np.concatenate([wq[:, :768][:, perm], wq[:, 768:]], axis=1)
        wB += _halfslabs_cols(wq_p)
        wB += _halfslabs_cols(inp["w_o"][l])
        wB += _mlp_slabs(inp["w_up"][l], inp["w_down"][l])
    wB = np.stack(wB)
    kTall = np.stack([np.asarray(RA[c]["kT"]) for c in range(NCORES)])
    vall = np.stack([np.asarray(RA[c]["v"]) for c in range(NCORES)])
    lamb = np.ascontiguousarray(np.broadcast_to(inp["b_lam"].reshape(1, 512), (128, 512))).astype(np.float32)
    kc = np.zeros((128, 128), np.float32)
    for col in range(128):
        kc[:, col] = (col * 128 + np.arange(128)) // 64
    in_maps = []
    for c in range(NCORES):
        tk = toks[c]
        in_maps.append({
            "wts": wB, "cst": cst, "memT": memT, "cmat": cmat,
            "xin": np.asarray(RA[c]["xout"]), "pos": np.ascontiguousarray(np.broadcast_to(tk.astype(np.float32)[None, :], (128, NTOK))),
            "kTall": kTall, "vall": vall, "lamb": lamb,
            "qc": np.ascontiguousarray(np.broadcast_to((tk // 64).astype(np.float32)[None, :], (128, NTOK))),
            "kc": kc,
        })
    if cores is not None:
        res = run_bass_kernel_spmd(_prog("B"), [in_maps[c] for c in cores], core_ids=list(range(len(cores))))
        return res.results
    resB = run_bass_kernel_spmd(_prog("B"), in_maps, core_ids=list(range(NCORES)))
    out = np.zeros((SEQ, D), np.float32)
    for c in range(NCORES):
        out[toks[c], :] = _unfeat(np.asarray(resB.results[c]["xout"]), NTOK).T
    return out[None]


def kernel(**inp):
    inp = {k: np.asarray(v) for k, v in inp.items()}
    RA = _run_A(inp)
    return _run_B(inp, RA)
```

```python
import numpy as np
import ml_dtypes
import concourse.bass as bass
import concourse.mybir as mybir
from concourse.bass_utils import run_bass_kernel_spmd

F32 = mybir.dt.float32
BF16 = mybir.dt.bfloat16
I32 = mybir.dt.int32
ALU = mybir.AluOpType
AF = mybir.ActivationFunctionType

NCORES = 8
D = 1024
SEQ = 16384
NCH = 8
T = 512
NTOK = 2048
NT = 4
BLK = 1024
MEM = 256
EPS = 1e-6
SUBLN_EPS = 1e-5
THETA = 500000.0
HS = 4096
NSLOT = 8
SAME_ENG_SYNC = True
DEBUG = None

CST = {}
_c = 0
def _add(name, n):
    global _c
    CST[name] = _c
    _c += n
for _l in range(4):
    _add(("norm_mix", _l), 8)
    _add(("norm_mlp", _l), 8)
_add("kv_norm", 8)
_add("mem_norm", 8)
for _l in range(2):
    for _k in range(3):
        _add(("conv", _l, _k), 6)
for _l in range(4):
    _add(("mem_q", _l), 1)
    _add(("mem_k", _l), 1)
for _j in range(2):
    _add(("b_q", _j), 1)
    _add(("subln", _j), 1)
_add("k_norm", 1)
_add("invf", 1)
_add("sgn", 1)
NCST = _c


class Sched:
    def __init__(self, nc):
        self.nc = nc
        self.order = ["pe", "act", "dve", "pool", "sp"]
        self.ops = {e: [] for e in self.order}
        self.state = {}
        self.nid = 0
        self.streams = {}

    def op(self, eng, fn, R=(), W=(), stream=None):
        o = {"eng": eng, "fn": fn, "deps": {}, "stream": stream, "id": self.nid, "used": False}
        self.nid += 1
        for k in R:
            st = self.state.get(k)
            if st is not None and st[0] is not None:
                o["deps"][st[0]["id"]] = st[0]
        for k in W:
            st = self.state.get(k)
            if st is not None:
                if st[0] is not None:
                    o["deps"][st[0]["id"]] = st[0]
                for r in st[1]:
                    o["deps"][r["id"]] = r
        for k in R:
            st = self.state.setdefault(k, [None, []])
            st[1].append(o)
        for k in W:
            self.state[k] = [o, []]
        self.ops[eng].append(o)
        return o

    def dma(self, eng, fn, R=(), W=(), stream="d"):
        return self.op(eng, fn, R, W, stream=stream)

    def finalize(self, sem_ctx):
        for e in self.order:
            for o in self.ops[e]:
                for d in o["deps"].values():
                    if d["stream"] is not None:
                        d["used"] = True
                    elif d["eng"] != o["eng"]:
                        d["used"] = True
                    elif SAME_ENG_SYNC and o["eng"] != "pe":
                        d["used"] = True
        engsem = {}
        for e in self.order:
            cnt = 0
            for o in self.ops[e]:
                if o["stream"] is not None:
                    s = self.streams.setdefault(o["stream"], [None, 0])
                    s[1] += 16
                    o["ticket"] = ("s", o["stream"], s[1])
                elif o["used"]:
                    cnt += 1
                    o["ticket"] = ("e", e, cnt)
        return

    def emit(self, block, sems):
        nc = self.nc
        handles = {"pe": block.tensor, "act": block.scalar, "dve": block.vector, "pool": block.gpsimd, "sp": block.sync}
        for e in self.order:
            ops = self.ops[e]
            if not ops:
                continue

            def body(h, ops=ops, e=e):
                seen = {}
                for o in ops:
                    for d in o["deps"].values():
                        if d["stream"] is None and d["eng"] == e and (e == "pe" or not SAME_ENG_SYNC):
                            continue
                        tk = d["ticket"]
                        key = tk[:2]
                        if seen.get(key, 0) >= tk[2]:
                            continue
                        seen[key] = tk[2]
                        h.wait_ge(sems[key], tk[2])
                    ins = o["fn"](h)
                    if o["stream"] is not None:
                        ins.then_inc(sems[("s", o["stream"])], 16)
                    elif o["used"]:
                        ins.then_inc(sems[("e", e)], 1)
                if e in ("sp", "pool"):
                    for o in ops:
                        pass
            handles[e](body)


def build_program(phase):
    nc = bass.Bass("TRN2", target_bir_lowering=False)
    S = Sched(nc)
    import contextlib
    es = contextlib.ExitStack()

    def dram(name, shape, dt, kind):
        return nc.dram_tensor(name, shape, dt, kind=kind).ap()

    def sb(name, shape, dt):
        return es.enter_context(nc.sbuf_tensor(name, shape, dt))

    def pst(name):
        return es.enter_context(nc.psum_tensor(name, [128, 512], F32))

    n_hs = 51 if phase == "A" else 42
    wts = dram("wts", [n_hs, 128, HS], F32, "ExternalInput")
    cst_d = dram("cst", [128, NCST], F32, "ExternalInput")
    memT_d = dram("memT", [128, 8 * MEM], F32, "ExternalInput")
    cmat_d = dram("cmat", [128, 4 * 128], F32, "ExternalInput")
    xin_d = dram("xin", [128, NCH * NTOK], F32, "ExternalInput")
    pos_d = dram("pos", [128, NTOK], F32, "ExternalInput")
    if phase == "A":
        xh_d = dram("xh", [128, NCH * 8], F32, "ExternalInput")
        hflag_d = dram("hflag", [128, 8], F32, "ExternalInput")
        xout_d = dram("xout", [128, NCH * NTOK], F32, "ExternalOutput")
        kT_d = dram("kT", [6, 128, NTOK], BF16, "ExternalOutput")
        v_d = dram("v", [NTOK, 768], BF16, "ExternalOutput")
    else:
        kTall_d = dram("kTall", [NCORES, 6, 128, NTOK], BF16, "ExternalInput")
        vall_d = dram("vall", [NCORES, NTOK, 768], BF16, "ExternalInput")
        lam_d = dram("lamb", [128, 512], F32, "ExternalInput")
        qc_d = dram("qc", [128, NTOK], F32, "ExternalInput")
        kc_d = dram("kc", [128, 128], F32, "ExternalInput")
        xout_d = dram("xout", [128, NCH * NTOK], F32, "ExternalOutput")

    x_t = sb("x", [128, NCH * NTOK], F32)
    NSL = NSLOT if phase == "A" else 6
    ring = [sb(f"w{i}", [128, HS], BF16) for i in range(NSL)]
    R_t = sb("R", [128, 16384], BF16)
    mm_t = sb("mainmo", [128, NCH * T], BF16)
    a_t = [sb(f"amlp{i}", [128, 4 * T], BF16) for i in range(2)]
    sq_t = [sb(f"sq{i}", [128, T], BF16) for i in range(2)]
    rs_t = [sb(f"rs{i}", [128, T], F32) for i in range(2)]
    sqf_t = [sb(f"sqf{i}", [128, T], F32) for i in range(2)]
    rd_t = sb("rd", [128, T], F32)
    pT_t = [sb(f"pT{i}", [128, T], BF16) for i in range(4)]
    cst = sb("cst_s", [128, NCST], F32)
    cmat_f = None
    cmat = sb("cmat_s", [128, 4 * 128], BF16)
    memTn = sb("memTn", [128, 8 * MEM], BF16)
    kmemT = sb("kmemT", [128, 2 * MEM], BF16)
    vmem = sb("vmem", [128, 2 * MEM], BF16)
    epsc = sb("epsc", [128, 2], F32)
    if phase == "A":
        xh_t = sb("xh_s", [128, NCH * 8], F32)
        hflag = sb("hflag_s", [128, 8], F32)
        uh_t = sb("uh", [128, 6 * 3 * 2], F32)
    else:
        lamt = sb("lamt", [128, 512], F32)
        lamv = sb("lamv", [128, 8], F32)
        kc_t = sb("kc_s", [128, 128], F32)
        qT_t = sb("qT", [128, 6 * T], BF16)
        kst = [sb(f"kst{i}", [128, 1024], BF16) for i in range(2)]
        vst = [sb(f"vst{i}", [128, 1024], BF16) for i in range(2)]
        osb = [sb(f"osb{i}", [128, T], F32) for i in range(2)]
    PS = [pst(f"ps{i}") for i in range(8)]

    ones_mean = cmat[:, 0:128]
    blockones = cmat[:, 128:256]
    ones_b = cmat[:, 256:384]
    perm_m = cmat[:, 384:512]

    def cc(name, off=0, n=1):
        b = CST[name] + off
        return cst[:, b:b + n]

    def Rv(off, nbytes, dt=BF16):
        ap = R_t[:, off // 2:(off + nbytes) // 2]
        if dt == F32:
            ap = ap.bitcast(F32)
        elif dt == I32:
            ap = ap.bitcast(I32)
        keys = [("R", g) for g in range(off // 1024, (off + nbytes + 1023) // 1024)]
        return ap, keys

    def h2_view(t, c, n=T):
        ap, k = Rv((t * 8 + c) * 1024, 1024)
        return ap[:, 0:n], k

    def h_view(buf, c, n=T):
        ap, k = Rv((buf * 8 + c) * 1024, 1024)
        return ap[:, 0:n], k

    SCR = 16384

    wstate = {"i": 0}

    def wload():
        i = wstate["i"]
        wstate["i"] += 1
        slot = i % NSL
        S.dma("pool", lambda h, i=i, slot=slot: h.dma_start(out=ring[slot][:], in_=wts[i]),
              W=[("w", slot)], stream=f"w{slot}")
        return slot

    def wcol(slot):
        return ring[slot][:].rearrange("p (k n) -> p k n", k=8)

    def wdown(slot):
        return ring[slot][:].rearrange("p (k n) -> p k n", k=4)

    pstate = {"i": 0}

    def bank():
        b = pstate["i"] % 8
        pstate["i"] += 1
        return b

    def mm(out_b, out_ap, lhsT, rhs, start, stop, R, extraW=()):
        S.op("pe", lambda h: h.matmul(out_ap, lhsT, rhs, start=start, stop=stop),
             R=R, W=[("ps", out_b)] + list(extraW))

    tog = {"sq": 0, "rs": 0, "sqf": 0, "pT": 0, "a": 0, "h": 0}

    def nxt(name, n):
        v = tog[name]
        tog[name] = (v + 1) % n
        return v

    def rstd_from(psb, n, eps_col, out_rs_i):
        rs = rs_t[out_rs_i]
        S.op("act", lambda h: h.activation(out=rs[:, :n], in_=PS[psb][:, :n], func=AF.Sqrt, bias=epsc[:, eps_col:eps_col + 1]),
             R=[("ps", psb), "epsc"], W=[("rs", out_rs_i)])
        S.op("dve", lambda h: h.reciprocal(rs[:, :n], rs[:, :n]), R=[("rs", out_rs_i)], W=[("rs", out_rs_i)])

    def norm_tile(xap, xkey, n, gname, hout):
        b = bank()
        for c in range(NCH):
            si = nxt("sq", 2)
            S.op("act", lambda h, c=c, si=si: h.activation(out=sq_t[si][:, :n], in_=xap(c), func=AF.Square),
                 R=[xkey(c)], W=[("sq", si)])
            mm(b, PS[b][:, :n], ones_mean, sq_t[si][:, :n], c == 0, c == NCH - 1, R=[("sq", si), "cmat"])
        ri = nxt("rs", 2)
        rstd_from(b, n, 0, ri)
        for c in range(NCH):
            hap, hk = hout(c)
            S.op("dve", lambda h, c=c, hap=hap: h.scalar_tensor_tensor(out=hap, in0=xap(c), scalar=cc(gname, c), in1=rs_t[ri][:, :n],
                                                                       op0=ALU.mult, op1=ALU.mult),
                 R=[xkey(c), ("rs", ri), "cst"], W=hk)

    def proj_chunk(b, n, slots, oc, hin, nk=NCH):
        slot = slots[oc // 4]
        co = (oc % 4) * 128
        for kc in range(nk):
            hap, hk = hin(kc)
            mm(b, PS[b][:, :n], wcol(slot)[:, kc, co:co + 128], hap, kc == 0, kc == nk - 1, R=hk + [("w", slot)])

    S.dma("sp", lambda h: h.dma_start(out=cst[:], in_=cst_d), W=["cst"], stream="c0")
    S.dma("pool", lambda h: h.dma_start(out=cmat[:], in_=cmat_d), W=["cmat"], stream="c1")
    x3 = x_t[:].rearrange("p (c n) -> p c n", c=NCH)
    xin3 = xin_d.rearrange("p (c n) -> p c n", c=NCH)
    for t in range(NT):
        S.dma("sp", lambda h, t=t: h.dma_start(out=x3[:, :, t * T:(t + 1) * T], in_=xin3[:, :, t * T:(t + 1) * T]),
              W=[("x", t, c) for c in range(NCH)], stream=f"x{t}")
    S.op("dve", lambda h: h.memset(epsc[:, 0:1], EPS), W=["epsc"])
    S.op("dve", lambda h: h.memset(epsc[:, 1:2], SUBLN_EPS), W=["epsc"])
    if phase == "A":
        S.dma("sp", lambda h: h.dma_start(out=xh_t[:], in_=xh_d), W=[("xh", c) for c in range(NCH)], stream="c2")
        S.dma("sp", lambda h: h.dma_start(out=hflag[:], in_=hflag_d), W=["hflag"], stream="c3")
    else:
        S.dma("sp", lambda h: h.dma_start(out=lamt[:], in_=lam_d), W=["lamt"], stream="c2")
        S.dma("sp", lambda h: h.dma_start(out=kc_t[:], in_=kc_d), W=["kc"], stream="c3")

    def xmain(t):
        return (lambda c: x_t[:, c * NTOK + t * T: c * NTOK + (t + 1) * T]), (lambda c: ("x", t, c))

    memf, memfk = Rv(SCR, 8192, F32)
    S.dma("sp", lambda h: h.dma_start(out=memf, in_=memT_d), W=memfk, stream="c4")
    norm_tile(lambda c: memf[:, c * MEM:(c + 1) * MEM], lambda c: memfk[c], MEM, "mem_norm",
              lambda c: (memTn[:, c * MEM:(c + 1) * MEM], ["memTn"]))

    def mem_kv(l, slot):
        w = wcol(slot)
        for c2 in range(2):
            b = bank()
            for kc in range(NCH):
                mm(b, PS[b][:, :MEM], w[:, kc, c2 * 128:(c2 + 1) * 128], memTn[:, kc * MEM:(kc + 1) * MEM], kc == 0, kc == NCH - 1,
                   R=["memTn", ("w", slot)])
            si = nxt("sq", 2)
            S.op("act", lambda h, b=b, si=si: h.activation(out=sq_t[si][:, :MEM], in_=PS[b][:, :MEM], func=AF.Square),
                 R=[("ps", b)], W=[("sq", si)])
            b2 = bank()
            mm(b2, PS[b2][:, :MEM], blockones, sq_t[si][:, :MEM], True, True, R=[("sq", si), "cmat"])
            ri = nxt("rs", 2)
            rstd_from(b2, MEM, 0, ri)
            S.op("dve", lambda h, b=b, c2=c2, ri=ri: h.scalar_tensor_tensor(out=kmemT[:, c2 * MEM:(c2 + 1) * MEM], in0=PS[b][:, :MEM],
                                                                            scalar=cc(("mem_k", l)), in1=rs_t[ri][:, :MEM],
                                                                            op0=ALU.mult, op1=ALU.mult),
                 R=[("ps", b), ("rs", ri), "cst"], W=["kmemT"])
        for mc in range(2):
            b = bank()
            for kc in range(NCH):
                mm(b, PS[b][:, :256], memTn[:, kc * MEM + mc * 128: kc * MEM + (mc + 1) * 128], w[:, kc, 256:512], kc == 0, kc == NCH - 1,
                   R=["memTn", ("w", slot)])
            S.op("act", lambda h, b=b, mc=mc: h.copy(out=vmem[:, mc * 256:(mc + 1) * 256], in_=PS[b][:, :256]),
                 R=[("ps", b)], W=["vmem"])

    def mem_attn(l, n, qproj, qn_view):
        for c2 in range(2):
            b = bank()
            qproj(c2, b)
            si = nxt("sq", 2)
            S.op("act", lambda h, b=b, si=si: h.activation(out=sq_t[si][:, :n], in_=PS[b][:, :n], func=AF.Square),
                 R=[("ps", b)], W=[("sq", si)])
            b2 = bank()
            mm(b2, PS[b2][:, :n], blockones, sq_t[si][:, :n], True, True, R=[("sq", si), "cmat"])
            ri = nxt("rs", 2)
            rstd_from(b2, n, 0, ri)
            qn, qnk = qn_view(c2)
            S.op("dve", lambda h, b=b, ri=ri, qn=qn: h.scalar_tensor_tensor(out=qn, in0=PS[b][:, :n], scalar=cc(("mem_q", l)),
                                                                            in1=rs_t[ri][:, :n], op0=ALU.mult, op1=ALU.mult),
                 R=[("ps", b), ("rs", ri), "cst"], W=qnk)
            bo = bank()
            bd = bank()
            for hh in range(2):
                r0 = 64 * hh
                for mc in range(2):
                    bs = bank()
                    mm(bs, PS[bs][:, :n], kmemT[r0:r0 + 64, c2 * MEM + mc * 128: c2 * MEM + (mc + 1) * 128], qn[r0:r0 + 64, :],
                       True, True, R=qnk + ["kmemT"])
                    pi = nxt("pT", 4)
                    S.op("act", lambda h, bs=bs, pi=pi: h.activation(out=pT_t[pi][:, :n], in_=PS[bs][:, :n], func=AF.Exp, scale=0.125),
                         R=[("ps", bs)], W=[("pT", pi)])
                    hcol = (2 * c2 + hh) * 64
                    mm(bo, PS[bo][r0:r0 + 64, :n], vmem[:, mc * 256 + hcol: mc * 256 + hcol + 64], pT_t[pi][:, :n], mc == 0, mc == 1,
                       R=[("pT", pi), "vmem"])
                    mm(bd, PS[bd][r0:r0 + 64, :n], ones_b[:, 0:64], pT_t[pi][:, :n], mc == 0, mc == 1, R=[("pT", pi), "cmat"])
            S.op("dve", lambda h, bd=bd: h.reciprocal(rd_t[:, :n], PS[bd][:, :n]), R=[("ps", bd)], W=["rd"])
            S.op("dve", lambda h, bo=bo, c2=c2: h.tensor_tensor(out=mm_t[:, (6 + c2) * T:(6 + c2) * T + n], in0=PS[bo][:, :n], in1=rd_t[:, :n],
                                                                op=ALU.mult),
                 R=[("ps", bo), "rd"], W=[("mm", 6 + c2)])

    def wo_apply(n, slots, xap, xkey):
        for oc in range(NCH):
            b = bank()
            proj_chunk(b, n, slots, oc, lambda kc: (mm_t[:, kc * T: kc * T + n], [("mm", kc)]))
            S.op("dve", lambda h, b=b, oc=oc: h.tensor_tensor(out=xap(oc), in0=xap(oc), in1=PS[b][:, :n], op=ALU.add),
                 R=[("ps", b), xkey(oc)], W=[xkey(oc)])

    def mlp(l, tiles):
        for (n, xap, xkey, ti) in tiles:
            norm_tile(xap, xkey, n, ("norm_mlp", l), lambda c, ti=ti, n=n: h2_view(ti, c, n))
        for g in range(8):
            su = wload()
            sd = wload()
            for (n, xap, xkey, ti) in tiles:
                ai = nxt("a", 2)
                for oc in range(4):
                    b = bank()
                    for kc in range(NCH):
                        hap, hk = h2_view(ti, kc, n)
                        mm(b, PS[b][:, :n], wcol(su)[:, kc, oc * 128:(oc + 1) * 128], hap, kc == 0, kc == NCH - 1, R=hk + [("w", su)])
                    fi = nxt("sqf", 2)
                    S.op("act", lambda h, b=b, fi=fi, n=n: h.activation(out=sqf_t[fi][:, :n], in_=PS[b][:, :n], func=AF.Square),
                         R=[("ps", b)], W=[("sqf", fi)])
                    S.op("dve", lambda h, b=b, fi=fi, ai=ai, oc=oc, n=n: h.scalar_tensor_tensor(
                        out=a_t[ai][:, oc * T: oc * T + n], in0=PS[b][:, :n], scalar=0.0, in1=sqf_t[fi][:, :n], op0=ALU.is_gt, op1=ALU.mult),
                        R=[("ps", b), ("sqf", fi)], W=[("a", ai, oc)])
                for oc in range(NCH):
                    b = bank()
                    for kc in range(4):
                        mm(b, PS[b][:, :n], wdown(sd)[:, kc, oc * 128:(oc + 1) * 128], a_t[ai][:, kc * T: kc * T + n], kc == 0, kc == 3,
                           R=[("a", ai, kc), ("w", sd)])
                    S.op("dve", lambda h, b=b, oc=oc, xap=xap, n=n: h.tensor_tensor(out=xap(oc), in0=xap(oc), in1=PS[b][:, :n], op=ALU.add),
                         R=[("ps", b), xkey(oc)], W=[xkey(oc)])

    def a_layer(l):
        smk = wload()
        mem_kv(l, smk)
        win = [wload() for _ in range(5)]
        wo = [wload() for _ in range(2)]
        tiles = [(8, (lambda c: xh_t[:, c * 8:(c + 1) * 8]), (lambda c: ("xh", c)), "h")]
        for t in range(NT):
            xa, xk = xmain(t)
            tiles.append((T, xa, xk, t))
        if DEBUG is not None:
            tiles = [tt for tt in tiles if tt[3] in DEBUG["tiles"]]
        for (n, xap, xkey, ti) in tiles:
            hb = nxt("h", 2)
            norm_tile(xap, xkey, n, ("norm_mix", l), lambda c, hb=hb, n=n: h_view(hb, c, n))
            if DEBUG is not None and ti == 0 and l == 0:
                for c in range(NCH):
                    hap, hk = h_view(hb, c, n)
                    S.dma("sp", lambda h, c=c, hap=hap: h.dma_start(out=dbg_h[:, c * T:(c + 1) * T], in_=hap), R=hk, stream=f"dbg{c}")
            hin = lambda kc, hb=hb, n=n: h_view(hb, kc, n)
            for j in range(6):
                par = j % 2
                hvs, hvk = Rv(SCR + par * 2048, 2048, F32)
                u, uk = Rv(SCR + 4096 + par * 2560, 2560, F32)
                y, yk = Rv(SCR + 9216 + par * 2048, 2048, F32)
                bh = bank()
                proj_chunk(bh, n, win, 12 + j, hin)
                S.op("act", lambda h, bh=bh, hvs=hvs, n=n: h.copy(out=hvs[:, :n], in_=PS[bh][:, :n]), R=[("ps", bh)], W=hvk)
                bc = bank()
                proj_chunk(bc, n, win, 6 + j, hin)
                if ti == "h":
                    S.op("dve", lambda h, u=u: h.memset(u[:, 0:2], 0.0), W=uk)
                else:
                    kind = 1 if ti == 0 else (2 if ti == 2 else 0)
                    S.op("dve", lambda h, u=u, j=j, kind=kind: h.tensor_copy(u[:, 0:2], uh_t[:, (j * 3 + kind) * 2:(j * 3 + kind) * 2 + 2]),
                         R=[("uh", j)], W=uk)
                S.op("dve", lambda h, bc=bc, u=u, hvs=hvs, n=n: h.tensor_tensor(out=u[:, 2:2 + n], in0=PS[bc][:, :n], in1=hvs[:, :n], op=ALU.mult),
                     R=[("ps", bc)] + hvk, W=uk)
                if ti == "h":
                    S.op("dve", lambda h, u=u: h.tensor_tensor(out=u[:, 2:10], in0=u[:, 2:10], in1=hflag[:, 0:8], op=ALU.mult),
                         R=uk + ["hflag"], W=uk)
                    S.op("dve", lambda h, u=u, j=j: h.tensor_copy(uh_t[:, (j * 3 + 1) * 2:(j * 3 + 1) * 2 + 2], u[:, 4:6]), R=uk, W=[("uh", j)])
                    S.op("dve", lambda h, u=u, j=j: h.tensor_copy(uh_t[:, (j * 3 + 2) * 2:(j * 3 + 2) * 2 + 2], u[:, 8:10]), R=uk, W=[("uh", j)])
                else:
                    S.op("dve", lambda h, u=u, j=j, n=n: h.tensor_copy(uh_t[:, (j * 3) * 2:(j * 3) * 2 + 2], u[:, n:n + 2]), R=uk, W=[("uh", j)])
                S.op("act", lambda h, u=u, y=y, j=j, n=n: h.activation(out=y[:, :n], in_=u[:, 2:2 + n], func=AF.Copy, scale=cc(("conv", l, 2), j)),
                     R=uk + ["cst"], W=yk)
                S.op("dve", lambda h, u=u, y=y, j=j, n=n: h.scalar_tensor_tensor(out=y[:, :n], in0=u[:, 1:1 + n], scalar=cc(("conv", l, 1), j),
                                                                               in1=y[:, :n], op0=ALU.mult, op1=ALU.add),
                     R=uk + yk + ["cst"], W=yk)
                S.op("dve", lambda h, u=u, y=y, j=j, n=n: h.scalar_tensor_tensor(out=y[:, :n], in0=u[:, 0:n], scalar=cc(("conv", l, 0), j),
                                                                               in1=y[:, :n], op0=ALU.mult, op1=ALU.add),
                     R=uk + yk + ["cst"], W=yk)
                bg = bank()
                proj_chunk(bg, n, win, j, hin)
                S.op("dve", lambda h, bg=bg, y=y, j=j, n=n: h.tensor_tensor(out=mm_t[:, j * T: j * T + n], in0=PS[bg][:, :n], in1=y[:, :n], op=ALU.mult),
                     R=[("ps", bg)] + yk, W=[("mm", j)])
            mem_attn(l, n, lambda c2, bq, n=n, hin=hin: proj_chunk(bq, n, win, 18 + c2, hin),
                     lambda c2, n=n: (lambda v: (v[0][:, :n], v[1]))(Rv(SCR + 13312 + c2 * 1024, 1024)))
            if DEBUG is not None and ti == 0 and l == 0:
                S.dma("sp", lambda h: h.dma_start(out=dbg_mm, in_=mm_t[:]), R=[("mm", c) for c in range(NCH)], stream="dbgmm")
            wo_apply(n, wo, xap, xkey)
            if DEBUG is not None and ti == 0 and l == 0:
                S.dma("sp", lambda h: h.dma_start(out=dbg_x1.rearrange("p (c n) -> p c n", c=NCH), in_=x3[:, :, 0:T]), R=[("x", 0, c) for c in range(NCH)], stream="dbgx1")
        mlp(l, tiles_for_mlp(tiles))

    def tiles_for_mlp(tiles):
        out = []
        for (n, xap, xkey, ti) in tiles:
            out.append((n, xap, xkey, 4 if ti == "h" else ti))
        return out

    h2h_t = sb("h2h", [128, NCH * 8], BF16)
    _h2_view_orig = h2_view

    def h2_view(t, c, n=T):
        if t == 4:
            return h2h_t[:, c * 8: c * 8 + n], [("h2h", c)]
        return _h2_view_orig(t, c, n)

    def rope_tables(t, Cap, Ck, Sap, Sk, tmp, tmpk, ki, kik, posb, posk):
        S.dma("sp", lambda h: h.dma_start(out=posb, in_=pos_d[:, t * T:(t + 1) * T]), W=posk, stream="pos")
        TWO_PI = 2.0 * np.pi
        C1 = 6.28125
        C2 = TWO_PI - C1
        for which, (oap, ok) in enumerate(((Sap, Sk), (Cap, Ck))):
            if which == 0:
                S.op("dve", lambda h, oap=oap: h.tensor_scalar(out=oap, in0=posb, scalar1=cc("invf"), scalar2=None, op0=ALU.mult),
                     R=posk + ["cst"], W=ok)
            else:
                S.op("dve", lambda h, oap=oap: h.tensor_scalar(out=oap, in0=posb, scalar1=cc("invf"), scalar2=float(np.pi / 2), op0=ALU.mult, op1=ALU.add),
                     R=posk + ["cst"], W=ok)
            S.op("dve", lambda h, oap=oap: h.tensor_scalar(out=tmp, in0=oap, scalar1=float(1.0 / TWO_PI), scalar2=None, op0=ALU.mult),
                 R=ok, W=tmpk)
            S.op("dve", lambda h: h.tensor_copy(ki, tmp), R=tmpk, W=kik)
            S.op("dve", lambda h: h.tensor_copy(tmp, ki), R=kik, W=tmpk)
            S.op("dve", lambda h, oap=oap: h.scalar_tensor_tensor(out=oap, in0=tmp, scalar=-C1, in1=oap, op0=ALU.mult, op1=ALU.add),
                 R=tmpk + ok, W=ok)
            S.op("dve", lambda h, oap=oap: h.scalar_tensor_tensor(out=oap, in0=tmp, scalar=-C2, in1=oap, op0=ALU.mult, op1=ALU.add),
                 R=tmpk + ok, W=ok)
            S.op("dve", lambda h, oap=oap: h.tensor_scalar(out=oap, in0=oap, scalar1=3.1415925, scalar2=-3.1415925, op0=ALU.min, op1=ALU.max),
                 R=ok, W=ok)
            S.op("act", lambda h, oap=oap: h.activation(out=oap, in_=oap, func=AF.Sin), R=ok, W=ok)
        S.op("dve", lambda h: h.tensor_scalar(out=Sap, in0=Sap, scalar1=cc("sgn"), scalar2=None, op0=ALU.mult), R=Sk + ["cst"], W=Sk)

    def head_norm_rope(b, n, gname, Cap, Ck, Sap, Sk, out_ap, out_k, tq, tqk, tb, tbk):
        si = nxt("sq", 2)
        S.op("act", lambda h: h.activation(out=sq_t[si][:, :n], in_=PS[b][:, :n], func=AF.Square), R=[("ps", b)], W=[("sq", si)])
        b2 = bank()
        mm(b2, PS[b2][:, :n], blockones, sq_t[si][:, :n], True, True, R=[("sq", si), "cmat"])
        ri = nxt("rs", 2)
        rstd_from(b2, n, 0, ri)
        S.op("dve", lambda h: h.scalar_tensor_tensor(out=tq, in0=PS[b][:, :n], scalar=cc(gname), in1=rs_t[ri][:, :n], op0=ALU.mult, op1=ALU.mult),
             R=[("ps", b), ("rs", ri), "cst"], W=tqk)
        S.op("act", lambda h: h.copy(out=tb, in_=tq), R=tqk, W=tbk)
        b3 = bank()
        mm(b3, PS[b3][:, :n], perm_m, tb, True, True, R=tbk + ["cmat"])
        S.op("dve", lambda h: h.tensor_tensor(out=tq, in0=tq, in1=Cap, op=ALU.mult), R=tqk + Ck, W=tqk)
        fi = nxt("sqf", 2)
        S.op("dve", lambda h: h.tensor_tensor(out=sqf_t[fi][:, :n], in0=PS[b3][:, :n], in1=Sap, op=ALU.mult), R=[("ps", b3)] + Sk, W=[("sqf", fi)])
        S.op("dve", lambda h: h.tensor_tensor(out=out_ap, in0=tq, in1=sqf_t[fi][:, :n], op=ALU.add), R=tqk + [("sqf", fi)], W=out_k)

    def kv_stage():
        wk = [wload() for _ in range(3)]
        Cap, Ck = Rv(SCR, 2048, F32)
        Sap, Sk = Rv(SCR + 2048, 2048, F32)
        tmp, tmpk = Rv(SCR + 4096, 2048, F32)
        ki, kik = Rv(SCR + 6144, 2048, I32)
        posb, posk = Rv(SCR + 8192, 2048, F32)
        tq, tqk = Rv(SCR + 10240, 2048, F32)
        tb, tbk = Rv(SCR + 12288, 1024, BF16)
        for t in range(NT):
            xa, xk = xmain(t)
            hb = nxt("h", 2)
            norm_tile(xa, xk, T, "kv_norm", lambda c, hb=hb: h_view(hb, c, T))
            hin = lambda kc, hb=hb: h_view(hb, kc, T)
            rope_tables(t, Cap, Ck, Sap, Sk, tmp, tmpk, ki, kik, posb, posk)
            for hd in range(6):
                b = bank()
                proj_chunk(b, T, wk, hd, hin)
                ko, kok = Rv(SCR + 13312 + (hd % 2) * 1024, 1024, BF16)
                head_norm_rope(b, T, "k_norm", Cap, Ck, Sap, Sk, ko, kok, tq, tqk, tb, tbk)
                S.dma("sp", lambda h, hd=hd, ko=ko, t=t: h.dma_start(out=kT_d[hd, :, t * T:(t + 1) * T], in_=ko), R=kok, stream=f"ko{hd % 2}")
            for s4 in range(4):
                vo = a_t[s4 % 2]
                for (c0, cn, wsl, wc0) in ((0, 256, wk[1], 256), (256, 512, wk[2], 0)):
                    b = bank()
                    for kc in range(NCH):
                        hap, hk = hin(kc)
                        mm(b, PS[b][:, :cn], hap[:, s4 * 128:(s4 + 1) * 128], wcol(wsl)[:, kc, wc0:wc0 + cn], kc == 0, kc == NCH - 1,
                           R=hk + [("w", wsl)])
                    S.op("act", lambda h, b=b, vo=vo, c0=c0, cn=cn: h.copy(out=vo[:, c0:c0 + cn], in_=PS[b][:, :cn]),
                         R=[("ps", b)], W=[("a", s4 % 2, 0), ("a", s4 % 2, 1)])
                S.dma("sp", lambda h, vo=vo, t=t, s4=s4: h.dma_start(out=v_d[t * T + s4 * 128: t * T + (s4 + 1) * 128, :], in_=vo[:, 0:768]),
                      R=[("a", s4 % 2, 0), ("a", s4 % 2, 1)], stream=f"vo{s4 % 2}")

    def b_layer(l):
        j = l - 2
        smk = wload()
        mem_kv(l, smk)
        wq = [wload() for _ in range(2)]
        wo = [wload() for _ in range(2)]
        lam_init = 0.8 - 0.6 * float(np.exp(-0.3 * l))
        lt = lamt[:, j * 256:(j + 1) * 256]
        S.op("dve", lambda h: h.tensor_tensor(out=sqf_t[0][:, 0:64], in0=lt[:, 0:64], in1=lt[:, 64:128], op=ALU.mult), R=["lamt"], W=[("sqf", 0)])
        S.op("dve", lambda h: h.tensor_tensor(out=sqf_t[0][:, 64:128], in0=lt[:, 128:192], in1=lt[:, 192:256], op=ALU.mult), R=["lamt", ("sqf", 0)], W=[("sqf", 0)])
        S.op("dve", lambda h: h.tensor_reduce(out=lamv[:, 2:4], in_=sqf_t[0][:, 0:128].rearrange("p (a b) -> p a b", a=2), axis=mybir.AxisListType.X, op=ALU.add),
             R=[("sqf", 0)], W=["lamv"])
        S.op("act", lambda h: h.activation(out=lamv[:, 4:6], in_=lamv[:, 2:4], func=AF.Exp), R=["lamv"], W=["lamv"])
        S.op("dve", lambda h: h.tensor_tensor(out=lamv[:, 0:1], in0=lamv[:, 4:5], in1=lamv[:, 5:6], op=ALU.subtract), R=["lamv"], W=["lamv"])
        S.op("dve", lambda h: h.tensor_scalar(out=lamv[:, 1:2], in0=lamv[:, 0:1], scalar1=lam_init, scalar2=-1.0, op0=ALU.add, op1=ALU.mult), R=["lamv"], W=["lamv"])

        Cap, Ck = Rv(SCR, 2048, F32)
        Sap, Sk = Rv(SCR + 2048, 2048, F32)
        tmp, tmpk = Rv(SCR + 4096, 2048, F32)
        ki, kik = Rv(SCR + 6144, 2048, I32)
        posb, posk = Rv(SCR + 8192, 2048, F32)
        tq, tqk = Rv(SCR + 10240, 2048, F32)
        tb, tbk = Rv(SCR + 12288, 1024, BF16)
        qcb, qck = Rv(SCR + 13312, 2048, F32)
        tiles = []
        for t in range(NT):
            if DEBUG is not None and t not in DEBUG["tiles"]:
                continue
            xa, xk = xmain(t)
            tiles.append((T, xa, xk, t))
            hb = nxt("h", 2)
            norm_tile(xa, xk, T, ("norm_mix", l), lambda c, hb=hb: h_view(hb, c, T))
            hin = lambda kc, hb=hb: h_view(hb, kc, T)
            rope_tables(t, Cap, Ck, Sap, Sk, tmp, tmpk, ki, kik, posb, posk)
            S.dma("sp", lambda h, t=t: h.dma_start(out=qcb, in_=qc_d[:, t * T:(t + 1) * T]), W=qck, stream="qc")
            for hd in range(6):
                b = bank()
                proj_chunk(b, T, wq, hd, hin)
                head_norm_rope(b, T, ("b_q", j), Cap, Ck, Sap, Sk, qT_t[:, hd * T:(hd + 1) * T], [("qT", hd)], tq, tqk, tb, tbk)
            mem_attn(l, T, lambda c2, bq, hin=hin: proj_chunk(bq, T, wq, 6 + c2, hin),
                     lambda c2: Rv(SCR + 8192 + c2 * 1024, 1024))
            if DEBUG is not None:
                S.dma("sp", lambda h: h.dma_start(out=dbg_q, in_=qT_t[:]), R=[("qT", hd) for hd in range(6)], stream="dbgq")
            diff_attn(l, j, t, qcb, qck, lam_init)
            if DEBUG is not None:
                S.dma("sp", lambda h: h.dma_start(out=dbg_mm, in_=mm_t[:]), R=[("mm", c) for c in range(NCH)], stream="dbgmm")
            wo_apply(T, wo, xa, xk)
            if DEBUG is not None:
                S.dma("sp", lambda h, t=t: h.dma_start(out=dbg_x1.rearrange("p (c n) -> p c n", c=NCH), in_=x3[:, :, t * T:(t + 1) * T]), R=[("x", t, c) for c in range(NCH)], stream="dbgx1")
        mlp(l, tiles)

    def diff_attn(l, j, t, qcb, qck, lam_init):
        NSB = 8 if t < 2 else 16
        for hd in range(6):
            bO1, bO2, bD1, bD2 = 4, 5, 6, 7
            first = True
            for sbk in range(NSB):
                r = sbk if sbk < 8 else 15 - sbk
                half = 0 if sbk < 8 else 1
                ks = (hd * NSB + sbk) % 2
                S.dma("sp", lambda h, r=r, half=half, ks=ks, hd=hd: h.dma_start(out=kst[ks][:], in_=kTall_d[r, hd, :, half * BLK:(half + 1) * BLK]),
                      W=[("kst", ks)], stream=f"kst{ks}")
                S.dma("sp", lambda h, r=r, half=half, ks=ks, hd=hd: h.dma_start(
                    out=vst[ks][:].rearrange("p (b e) -> p b e", b=8),
                    in_=vall_d[r, half * BLK:(half + 1) * BLK, hd * 128:(hd + 1) * 128].rearrange("(b p) e -> p b e", p=128)),
                    W=[("vst", ks)], stream=f"vst{ks}")
                for kb in range(8):
                    kcol = sbk * 8 + kb
                    bS = [(2 * kb) % 4, (2 * kb + 1) % 4]
                    pis = []
                    for m in range(2):
                        r0 = 64 * m
                        mm(bS[m], PS[bS[m]][:, :T], kst[ks][r0:r0 + 64, kb * 128:(kb + 1) * 128], qT_t[r0:r0 + 64, hd * T:(hd + 1) * T],
                           True, True, R=[("kst", ks), ("qT", hd)])
                    for m in range(2):
                        pi = nxt("pT", 4)
                        pis.append(pi)
                        S.op("act", lambda h, m=m, pi=pi, bS=bS: h.activation(out=pT_t[pi][:], in_=PS[bS[m]][:, :T], func=AF.Exp, scale=0.125),
                             R=[("ps", bS[m])], W=[("pT", pi)])
                        if not (t >= 2 and sbk < 8):
                            S.op("dve", lambda h, pi=pi, kcol=kcol: h.scalar_tensor_tensor(out=pT_t[pi][:], in0=qcb, scalar=kc_t[:, kcol:kcol + 1],
                                                                                          in1=pT_t[pi][:], op0=ALU.is_ge, op1=ALU.mult),
                                 R=qck + ["kc", ("pT", pi)], W=[("pT", pi)])
                    last = (sbk == NSB - 1 and kb == 7)
                    for m, (bo, bd) in enumerate(((bO1, bD1), (bO2, bD2))):
                        mm(bo, PS[bo][:, :T], vst[ks][:, kb * 128:(kb + 1) * 128], pT_t[pis[m]][:], first, last, R=[("pT", pis[m]), ("vst", ks)])
                        mm(bd, PS[bd][:, :T], ones_b, pT_t[pis[m]][:], first, last, R=[("pT", pis[m]), "cmat"])
                    first = False
            o1, o2 = osb[0], osb[1]
            S.op("dve", lambda h: h.reciprocal(rd_t[:], PS[bD1][:, :T]), R=[("ps", bD1)], W=["rd"])
            S.op("dve", lambda h: h.tensor_tensor(out=o1[:], in0=PS[bO1][:, :T], in1=rd_t[:], op=ALU.mult), R=[("ps", bO1), "rd"], W=[("osb", 0)])
            S.op("dve", lambda h: h.reciprocal(rd_t[:], PS[bD2][:, :T]), R=[("ps", bD2), ("osb", 0)], W=["rd"])
            S.op("dve", lambda h: h.tensor_tensor(out=o2[:], in0=PS[bO2][:, :T], in1=rd_t[:], op=ALU.mult), R=[("ps", bO2), "rd"], W=[("osb", 1)])
            S.op("dve", lambda h: h.scalar_tensor_tensor(out=o1[:], in0=o2[:], scalar=lamv[:, 1:2], in1=o1[:], op0=ALU.mult, op1=ALU.add),
                 R=[("osb", 0), ("osb", 1), "lamv"], W=[("osb", 0)])
            si = nxt("sq", 2)
            S.op("act", lambda h, si=si: h.activation(out=sq_t[si][:], in_=o1[:], func=AF.Square), R=[("osb", 0)], W=[("sq", si)])
            b2 = bank() % 4
            mm(b2, PS[b2][:, :T], ones_mean, sq_t[si][:], True, True, R=[("sq", si), "cmat"])
            ri = nxt("rs", 2)
            S.op("act", lambda h, b2=b2, ri=ri: h.activation(out=rs_t[ri][:], in_=PS[b2][:, :T], func=AF.Sqrt, bias=epsc[:, 1:2], scale=8.0),
                 R=[("ps", b2), "epsc"], W=[("rs", ri)])
            S.op("dve", lambda h, ri=ri: h.reciprocal(rs_t[ri][:], rs_t[ri][:]), R=[("rs", ri)], W=[("rs", ri)])
            S.op("dve", lambda h, ri=ri: h.scalar_tensor_tensor(out=o1[:], in0=o1[:], scalar=cc(("subln", j)), in1=rs_t[ri][:], op0=ALU.mult, op1=ALU.mult),
                 R=[("osb", 0), ("rs", ri), "cst"], W=[("osb", 0)])
            S.op("act", lambda h, hd=hd: h.activation(out=mm_t[:, hd * T:(hd + 1) * T], in_=o1[:], func=AF.Copy, scale=float(1.0 - lam_init)),
                 R=[("osb", 0)], W=[("mm", hd)])

    if phase == "A" and DEBUG is not None:
        dbg_h = dram("dbg_h", [128, NCH * T], BF16, "ExternalOutput")
        dbg_mm = dram("dbg_mm", [128, NCH * T], BF16, "ExternalOutput")
        dbg_x1 = dram("dbg_x1", [128, NCH * T], F32, "ExternalOutput")
        for l in DEBUG["layers"]:
            a_layer(l)
    elif phase == "A":
        a_layer(0)
        a_layer(1)
        kv_stage()
    elif DEBUG is not None:
        dbg_q = dram("dbg_q", [128, 6 * T], BF16, "ExternalOutput")
        dbg_mm = dram("dbg_mm", [128, NCH * T], BF16, "ExternalOutput")
        dbg_x1 = dram("dbg_x1", [128, NCH * T], F32, "ExternalOutput")
        for l in DEBUG["layers"]:
            b_layer(l)
    else:
        b_layer(2)
        b_layer(3)
    xout3 = xout_d.rearrange("p (c n) -> p c n", c=NCH)
    for t in range(NT):
        S.dma("sp", lambda h, t=t: h.dma_start(out=xout3[:, :, t * T:(t + 1) * T], in_=x3[:, :, t * T:(t + 1) * T]),
              R=[("x", t, c) for c in range(NCH)], stream="xo")
    outs = [o for o in S.ops["sp"] if o["stream"] is not None and (o["stream"].startswith("xo") or o["stream"].startswith("ko") or o["stream"].startswith("vo"))]
    last_by_stream = {}
    for o in outs:
        last_by_stream[o["stream"]] = o
    fin = S.op("sp", lambda h: h.nop(), R=(), W=())
    for o in last_by_stream.values():
        fin["deps"][o["id"]] = o

    S.finalize(None)
    sems = {}
    for e in S.order:
        sems[("e", e)] = es.enter_context(nc.semaphore(f"sem_{e}"))
    for sname in S.streams:
        sems[("s", sname)] = es.enter_context(nc.semaphore(f"ds_{sname}"))
    block = es.enter_context(nc.Block())
    S.emit(block, sems)
    es.close()
    return nc


def _pk(w):
    K, N = w.shape
    return np.ascontiguousarray(w.reshape(K // 128, 128, N).transpose(1, 0, 2))


def _halfslabs_cols(w):
    out = []
    for c0 in range(0, w.shape[1], 512):
        out.append(_pk(w[:, c0:c0 + 512]).reshape(128, HS))
    return out


def _halfslab_rows(w):
    return _pk(w).reshape(128, HS)


def _mlp_slabs(w_up, w_down):
    out = []
    for g in range(8):
        out.append(_pk(w_up[:, g * 512:(g + 1) * 512]).reshape(128, HS))
        out.append(_halfslab_rows(w_down[g * 512:(g + 1) * 512, :]))
    return out


def _col128(v):
    return np.ascontiguousarray(v.reshape(-1, 128).T)


def _build_cst(inp):
    cst = np.zeros((128, NCST), np.float32)
    for l in range(4):
        cst[:, CST[("norm_mix", l)]:CST[("norm_mix", l)] + 8] = _col128(inp["norm_mix"][l])
        cst[:, CST[("norm_mlp", l)]:CST[("norm_mlp", l)] + 8] = _col128(inp["norm_mlp"][l])
        cst[:, CST[("mem_q", l)]] = np.tile(inp["mem_q_norm"][l], 2)
        cst[:, CST[("mem_k", l)]] = np.tile(inp["mem_k_norm"][l], 2)
    cst[:, CST["kv_norm"]:CST["kv_norm"] + 8] = _col128(inp["kv_norm"])
    cst[:, CST["mem_norm"]:CST["mem_norm"] + 8] = _col128(inp["mem_norm"])
    for l in range(2):
        for k in range(3):
            cst[:, CST[("conv", l, k)]:CST[("conv", l, k)] + 6] = _col128(inp["a_conv"][l, k])
    for j in range(2):
        cst[:, CST[("b_q", j)]] = np.tile(inp["b_q_norm"][j], 2)
        cst[:, CST[("subln", j)]] = inp["b_subln"][j]
    cst[:, CST["k_norm"]] = np.tile(inp["k_norm"], 2)
    d = np.arange(128) % 64
    invf = np.where(d < 16, 1.0 / (np.float32(THETA) ** (np.arange(0, 16, 2, dtype=np.float32) / np.float32(16)))[d % 8], 0.0)
    cst[:, CST["invf"]] = invf.astype(np.float32)
    cst[:, CST["sgn"]] = np.where(d < 8, -1.0, np.where(d < 16, 1.0, 0.0))
    return cst


def _build_cmat():
    cm = np.zeros((128, 4 * 128), np.float32)
    cm[:, 0:128] = 1.0 / 1024.0
    for b in range(2):
        cm[b * 64:(b + 1) * 64, 128 + b * 64:128 + (b + 1) * 64] = 1.0 / 64.0
    cm[:, 256:384] = 1.0
    for m in range(128):
        d = m % 64
        if d < 8:
            cm[m + 8, 384 + m] = 1.0
        elif d < 16:
            cm[m - 8, 384 + m] = 1.0
    return cm


def _core_tokens(c):
    a = np.arange(c * BLK, (c + 1) * BLK)
    b = np.arange((15 - c) * BLK, (16 - c) * BLK)
    return np.concatenate([a, b])


def _qk_perm():
    idx = []
    for h in range(6):
        idx += list(range(h * 64, (h + 1) * 64))
        idx += list(range(384 + h * 64, 384 + (h + 1) * 64))
    return np.array(idx)


_PROGS = {}


def _prog(phase):
    if phase not in _PROGS:
        _PROGS[phase] = build_program(phase)
    return _PROGS[phase]


def _featmajor(xT_core):
    n = xT_core.shape[1]
    return np.ascontiguousarray(xT_core.reshape(8, 128, n).transpose(1, 0, 2).reshape(128, 8 * n))


def _unfeat(a, n):
    return a.reshape(128, 8, n).transpose(1, 0, 2).reshape(1024, n)


def _run_A(inp):
    x = inp["x"][0]
    xT = np.ascontiguousarray(x.T)
    memT = _featmajor(np.ascontiguousarray(inp["mem"][0].T))
    cst = _build_cst(inp)
    cmat = _build_cmat()
    perm = _qk_perm()

    wA = []
    for l in range(2):
        wA += _halfslabs_cols(inp["w_mem_kv"][l])
        wA += _halfslabs_cols(inp["a_w_in"][l])
        wA += _halfslabs_cols(inp["w_o"][l])
        wA += _mlp_slabs(inp["w_up"][l], inp["w_down"][l])
    wkv = inp["w_kv"]
    wkv_p = np.concatenate([wkv[:, :768][:, perm], wkv[:, 768:]], axis=1)
    wA += _halfslabs_cols(wkv_p)
    wA = np.stack(wA)
    in_maps = []
    toks = [_core_tokens(c) for c in range(NCORES)]
    for c in range(NCORES):
        tk = toks[c]
        xh = np.zeros((1024, 8), np.float32)
        hf = np.ones((128, 8), np.float32)
        for bi, b0 in enumerate((c * BLK, (15 - c) * BLK)):
            if b0 >= 4:
                xh[:, bi * 4:(bi + 1) * 4] = xT[:, b0 - 4:b0]
            else:
                hf[:, bi * 4:(bi + 1) * 4] = 0.0
        in_maps.append({
            "wts": wA, "cst": cst, "memT": memT, "cmat": cmat,
            "xin": _featmajor(xT[:, tk]), "pos": np.ascontiguousarray(np.broadcast_to(tk.astype(np.float32)[None, :], (128, NTOK))),
            "xh": _featmajor(xh), "hflag": hf,
        })
    resA = run_bass_kernel_spmd(_prog("A"), in_maps, core_ids=list(range(NCORES)))
    return resA.results


def _run_B(inp, RA, cores=None):
    memT = _featmajor(np.ascontiguousarray(inp["mem"][0].T))
    cst = _build_cst(inp)
    cmat = _build_cmat()
    perm = _qk_perm()
    toks = [_core_tokens(c) for c in range(NCORES)]
    wB = []
    for l in range(2, 4):
        j = l - 2
        wB += _halfslabs_cols(inp["w_mem_kv"][l])
        wq = inp["b_w_q"][j]
        wq_p = np.concatenate([wq[:, :768][:, perm], wq[:, 768:]], axis=1)
        wB += _halfslabs_cols(wq_p)
        wB += _halfslabs_cols(inp["w_o"][l])
        wB += _mlp_slabs(inp["w_up"][l], inp["w_down"][l])
    wB = np.stack(wB)
    kTall = np.stack([np.asarray(RA[c]["kT"]) for c in range(NCORES)])
    vall = np.stack([np.asarray(RA[c]["v"]) for c in range(NCORES)])
    lamb = np.ascontiguousarray(np.broadcast_to(inp["b_lam"].reshape(1, 512), (128, 512))).astype(np.float32)
    kc = np.zeros((128, 128), np.float32)
    for col in range(128):
        kc[:, col] = (col * 128 + np.arange(128)) // 64
    in_maps = []
    for c in range(NCORES):
        tk = toks[c]
        in_maps.append({
            "wts": wB, "cst": cst, "memT": memT, "cmat": cmat,
            "xin": np.asarray(RA[c]["xout"]), "pos": np.ascontiguousarray(np.broadcast_to(tk.astype(np.float32)[None, :], (128, NTOK))),
            "kTall": kTall, "vall": vall, "lamb": lamb,
            "qc": np.ascontiguousarray(np.broadcast_to((tk // 64).astype(np.float32)[None, :], (128, NTOK))),
            "kc": kc,
        })
    if cores is not None:
        res = run_bass_kernel_spmd(_prog("B"), [in_maps[c] for c in cores], core_ids=list(range(len(cores))))
        return res.results
    resB = run_bass_kernel_spmd(_prog("B"), in_maps, core_ids=list(range(NCORES)))
    out = np.zeros((SEQ, D), np.float32)
    for c in range(NCORES):
        out[toks[c], :] = _unfeat(np.asarray(resB.results[c]["xout"]), NTOK).T
    return out[None]


def kernel(**inp):
    inp = {k: np.asarray(v) for k, v in inp.items()}
    RA = _run_A(inp)
    return _run_B(inp, RA)
```

```python
import numpy as np
import ml_dtypes
import concourse.bass as bass
import concourse.mybir as mybir
from concourse.bass_utils import run_bass_kernel_spmd

F32 = mybir.dt.float32
BF16 = mybir.dt.bfloat16
I32 = mybir.dt.int32
ALU = mybir.AluOpType
AF = mybir.ActivationFunctionType

NCORES = 8
D = 1024
SEQ = 16384
NCH = 8
T = 512
NTOK = 2048
NT = 4
BLK = 1024
MEM = 256
EPS = 1e-6
SUBLN_EPS = 1e-5
THETA = 500000.0
HS = 4096
NSLOT = 8
SAME_ENG_SYNC = True
DEBUG = None

CST = {}
_c = 0
def _add(name, n):
    global _c
    CST[name] = _c
    _c += n
for _l in range(4):
    _add(("norm_mix", _l), 8)
    _add(("norm_mlp", _l), 8)
_add("kv_norm", 8)
_add("mem_norm", 8)
for _l in range(2):
    for _k in range(3):
        _add(("conv", _l, _k), 6)
for _l in range(4):
    _add(("mem_q", _l), 1)
    _add(("mem_k", _l), 1)
for _j in range(2):
    _add(("b_q", _j), 1)
    _add(("subln", _j), 1)
_add("k_norm", 1)
_add("invf", 1)
_add("sgn", 1)
NCST = _c


class Sched:
    def __init__(self, nc):
        self.nc = nc
        self.order = ["pe", "act", "dve", "pool", "sp"]
        self.ops = {e: [] for e in self.order}
        self.state = {}
        self.nid = 0
        self.streams = {}

    def op(self, eng, fn, R=(), W=(), stream=None):
        o = {"eng": eng, "fn": fn, "deps": {}, "stream": stream, "id": self.nid, "used": False}
        self.nid += 1
        for k in R:
            st = self.state.get(k)
            if st is not None and st[0] is not None:
                o["deps"][st[0]["id"]] = st[0]
        for k in W:
            st = self.state.get(k)
            if st is not None:
                if st[0] is not None:
                    o["deps"][st[0]["id"]] = st[0]
                for r in st[1]:
                    o["deps"][r["id"]] = r
        for k in R:
            st = self.state.setdefault(k, [None, []])
            st[1].append(o)
        for k in W:
            self.state[k] = [o, []]
        self.ops[eng].append(o)
        return o

    def dma(self, eng, fn, R=(), W=(), stream="d"):
        return self.op(eng, fn, R, W, stream=stream)

    def finalize(self, sem_ctx):
        for e in self.order:
            for o in self.ops[e]:
                for d in o["deps"].values():
                    if d["stream"] is not None:
                        d["used"] = True
                    elif d["eng"] != o["eng"]:
                        d["used"] = True
                    elif SAME_ENG_SYNC and o["eng"] != "pe":
                        d["used"] = True
        engsem = {}
        for e in self.order:
            cnt = 0
            for o in self.ops[e]:
                if o["stream"] is not None:
                    s = self.streams.setdefault(o["stream"], [None, 0])
                    s[1] += 16
                    o["ticket"] = ("s", o["stream"], s[1])
                elif o["used"]:
                    cnt += 1
                    o["ticket"] = ("e", e, cnt)
        return

    def emit(self, block, sems):
        nc = self.nc
        handles = {"pe": block.tensor, "act": block.scalar, "dve": block.vector, "pool": block.gpsimd, "sp": block.sync}
        for e in self.order:
            ops = self.ops[e]
            if not ops:
                continue

            def body(h, ops=ops, e=e):
                seen = {}
                for o in ops:
                    for d in o["deps"].values():
                        if d["stream"] is None and d["eng"] == e and (e == "pe" or not SAME_ENG_SYNC):
                            continue
                        tk = d["ticket"]
                        key = tk[:2]
                        if seen.get(key, 0) >= tk[2]:
                            continue
                        seen[key] = tk[2]
                        h.wait_ge(sems[key], tk[2])
                    ins = o["fn"](h)
                    if o["stream"] is not None:
                        ins.then_inc(sems[("s", o["stream"])], 16)
                    elif o["used"]:
                        ins.then_inc(sems[("e", e)], 1)
                if e in ("sp", "pool"):
                    for o in ops:
                        pass
            handles[e](body)


def build_program(phase):
    nc = bass.Bass("TRN2", target_bir_lowering=False)
    S = Sched(nc)
    import contextlib
    es = contextlib.ExitStack()

    def dram(name, shape, dt, kind):
        return nc.dram_tensor(name, shape, dt, kind=kind).ap()

    def sb(name, shape, dt):
        return es.enter_context(nc.sbuf_tensor(name, shape, dt))

    def pst(name):
        return es.enter_context(nc.psum_tensor(name, [128, 512], F32))

    n_hs = 51 if phase == "A" else 42
    wts = dram("wts", [n_hs, 128, HS], F32, "ExternalInput")
    cst_d = dram("cst", [128, NCST], F32, "ExternalInput")
    memT_d = dram("memT", [128, 8 * MEM], F32, "ExternalInput")
    cmat_d = dram("cmat", [128, 4 * 128], F32, "ExternalInput")
    xin_d = dram("xin", [128, NCH * NTOK], F32, "ExternalInput")
    pos_d = dram("pos", [128, NTOK], F32, "ExternalInput")
    if phase == "A":
        xh_d = dram("xh", [128, NCH * 8], F32, "ExternalInput")
        hflag_d = dram("hflag", [128, 8], F32, "ExternalInput")
        xout_d = dram("xout", [128, NCH * NTOK], F32, "ExternalOutput")
        kT_d = dram("kT", [6, 128, NTOK], BF16, "ExternalOutput")
        v_d = dram("v", [NTOK, 768], BF16, "ExternalOutput")
    else:
        kTall_d = dram("kTall", [NCORES, 6, 128, NTOK], BF16, "ExternalInput")
        vall_d = dram("vall", [NCORES, NTOK, 768], BF16, "ExternalInput")
        lam_d = dram("lamb", [128, 512], F32, "ExternalInput")
        qc_d = dram("qc", [128, NTOK], F32, "ExternalInput")
        kc_d = dram("kc", [128, 128], F32, "ExternalInput")
        xout_d = dram("xout", [128, NCH * NTOK], F32, "ExternalOutput")

    x_t = sb("x", [128, NCH * NTOK], F32)
    NSL = NSLOT if phase == "A" else 6
    ring = [sb(f"w{i}", [128, HS], BF16) for i in range(NSL)]
    R_t = sb("R", [128, 16384], BF16)
    mm_t = sb("mainmo", [128, NCH * T], BF16)
    a_t = [sb(f"amlp{i}", [128, 4 * T], BF16) for i in range(2)]
    sq_t = [sb(f"sq{i}", [128, T], BF16) for i in range(2)]
    rs_t = [sb(f"rs{i}", [128, T], F32) for i in range(2)]
    sqf_t = [sb(f"sqf{i}", [128, T], F32) for i in range(2)]
    rd_t = sb("rd", [128, T], F32)
    pT_t = [sb(f"pT{i}", [128, T], BF16) for i in range(4)]
    cst = sb("cst_s", [128, NCST], F32)
    cmat_f = None
    cmat = sb("cmat_s", [128, 4 * 128], BF16)
    memTn = sb("memTn", [128, 8 * MEM], BF16)
    kmemT = sb("kmemT", [128, 2 * MEM], BF16)
    vmem = sb("vmem", [128, 2 * MEM], BF16)
    epsc = sb("epsc", [128, 2], F32)
    if phase == "A":
        xh_t = sb("xh_s", [128, NCH * 8], F32)
        hflag = sb("hflag_s", [128, 8], F32)
        uh_t = sb("uh", [128, 6 * 3 * 2], F32)
    else:
        lamt = sb("lamt", [128, 512], F32)
        lamv = sb("lamv", [128, 8], F32)
        kc_t = sb("kc_s", [128, 128], F32)
        qT_t = sb("qT", [128, 6 * T], BF16)
        kst = [sb(f"kst{i}", [128, 1024], BF16) for i in range(2)]
        vst = [sb(f"vst{i}", [128, 1024], BF16) for i in range(2)]
        osb = [sb(f"osb{i}", [128, T], F32) for i in range(2)]
    PS = [pst(f"ps{i}") for i in range(8)]

    ones_mean = cmat[:, 0:128]
    blockones = cmat[:, 128:256]
    ones_b = cmat[:, 256:384]
    perm_m = cmat[:, 384:512]

    def cc(name, off=0, n=1):
        b = CST[name] + off
        return cst[:, b:b + n]

    def Rv(off, nbytes, dt=BF16):
        ap = R_t[:, off // 2:(off + nbytes) // 2]
        if dt == F32:
            ap = ap.bitcast(F32)
        elif dt == I32:
            ap = ap.bitcast(I32)
        keys = [("R", g) for g in range(off // 1024, (off + nbytes + 1023) // 1024)]
        return ap, keys

    def h2_view(t, c, n=T):
        ap, k = Rv((t * 8 + c) * 1024, 1024)
        return ap[:, 0:n], k

    def h_view(buf, c, n=T):
        ap, k = Rv((buf * 8 + c) * 1024, 1024)
        return ap[:, 0:n], k

    SCR = 16384

    wstate = {"i": 0}

    def wload():
        i = wstate["i"]
        wstate["i"] += 1
        slot = i % NSL
        S.dma("pool", lambda h, i=i, slot=slot: h.dma_start(out=ring[slot][:], in_=wts[i]),
              W=[("w", slot)], stream=f"w{slot}")
        return slot

    def wcol(slot):
        return ring[slot][:].rearrange("p (k n) -> p k n", k=8)

    def wdown(slot):
        return ring[slot][:].rearrange("p (k n) -> p k n", k=4)

    pstate = {"i": 0}

    def bank():
        b = pstate["i"] % 8
        pstate["i"] += 1
        return b

    def mm(out_b, out_ap, lhsT, rhs, start, stop, R, extraW=()):
        S.op("pe", lambda h: h.matmul(out_ap, lhsT, rhs, start=start, stop=stop),
             R=R, W=[("ps", out_b)] + list(extraW))

    tog = {"sq": 0, "rs": 0, "sqf": 0, "pT": 0, "a": 0, "h": 0}

    def nxt(name, n):
        v = tog[name]
        tog[name] = (v + 1) % n
        return v

    def rstd_from(psb, n, eps_col, out_rs_i):
        rs = rs_t[out_rs_i]
        S.op("act", lambda h: h.activation(out=rs[:, :n], in_=PS[psb][:, :n], func=AF.Sqrt, bias=epsc[:, eps_col:eps_col + 1]),
             R=[("ps", psb), "epsc"], W=[("rs", out_rs_i)])
        S.op("dve", lambda h: h.reciprocal(rs[:, :n], rs[:, :n]), R=[("rs", out_rs_i)], W=[("rs", out_rs_i)])

    def norm_tile(xap, xkey, n, gname, hout):
        b = bank()
        for c in range(NCH):
            si = nxt("sq", 2)
            S.op("act", lambda h, c=c, si=si: h.activation(out=sq_t[si][:, :n], in_=xap(c), func=AF.Square),
                 R=[xkey(c)], W=[("sq", si)])
            mm(b, PS[b][:, :n], ones_mean, sq_t[si][:, :n], c == 0, c == NCH - 1, R=[("sq", si), "cmat"])
        ri = nxt("rs", 2)
        rstd_from(b, n, 0, ri)
        for c in range(NCH):
            hap, hk = hout(c)
            S.op("dve", lambda h, c=c, hap=hap: h.scalar_tensor_tensor(out=hap, in0=xap(c), scalar=cc(gname, c), in1=rs_t[ri][:, :n],
                                                                       op0=ALU.mult, op1=ALU.mult),
                 R=[xkey(c), ("rs", ri), "cst"], W=hk)

    def proj_chunk(b, n, slots, oc, hin, nk=NCH):
        slot = slots[oc // 4]
        co = (oc % 4) * 128
        for kc in range(nk):
            hap, hk = hin(kc)
            mm(b, PS[b][:, :n], wcol(slot)[:, kc, co:co + 128], hap, kc == 0, kc == nk - 1, R=hk + [("w", slot)])

    S.dma("sp", lambda h: h.dma_start(out=cst[:], in_=cst_d), W=["cst"], stream="c0")
    S.dma("pool", lambda h: h.dma_start(out=cmat[:], in_=cmat_d), W=["cmat"], stream="c1")
    x3 = x_t[:].rearrange("p (c n) -> p c n", c=NCH)
    xin3 = xin_d.rearrange("p (c n) -> p c n", c=NCH)
    for t in range(NT):
        S.dma("sp", lambda h, t=t: h.dma_start(out=x3[:, :, t * T:(t + 1) * T], in_=xin3[:, :, t * T:(t + 1) * T]),
              W=[("x", t, c) for c in range(NCH)], stream=f"x{t}")
    S.op("dve", lambda h: h.memset(epsc[:, 0:1], EPS), W=["epsc"])
    S.op("dve", lambda h: h.memset(epsc[:, 1:2], SUBLN_EPS), W=["epsc"])
    if phase == "A":
        S.dma("sp", lambda h: h.dma_start(out=xh_t[:], in_=xh_d), W=[("xh", c) for c in range(NCH)], stream="c2")
        S.dma("sp", lambda h: h.dma_start(out=hflag[:], in_=hflag_d), W=["hflag"], stream="c3")
    else:
        S.dma("sp", lambda h: h.dma_start(out=lamt[:], in_=lam_d), W=["lamt"], stream="c2")
        S.dma("sp", lambda h: h.dma_start(out=kc_t[:], in_=kc_d), W=["kc"], stream="c3")

    def xmain(t):
        return (lambda c: x_t[:, c * NTOK + t * T: c * NTOK + (t + 1) * T]), (lambda c: ("x", t, c))

    memf, memfk = Rv(SCR, 8192, F32)
    S.dma("sp", lambda h: h.dma_start(out=memf, in_=memT_d), W=memfk, stream="c4")
    norm_tile(lambda c: memf[:, c * MEM:(c + 1) * MEM], lambda c: memfk[c], MEM, "mem_norm",
              lambda c: (memTn[:, c * MEM:(c + 1) * MEM], ["memTn"]))

    def mem_kv(l, slot):
        w = wcol(slot)
        for c2 in range(2):
            b = bank()
            for kc in range(NCH):
                mm(b, PS[b][:, :MEM], w[:, kc, c2 * 128:(c2 + 1) * 128], memTn[:, kc * MEM:(kc + 1) * MEM], kc == 0, kc == NCH - 1,
                   R=["memTn", ("w", slot)])
            si = nxt("sq", 2)
            S.op("act", lambda h, b=b, si=si: h.activation(out=sq_t[si][:, :MEM], in_=PS[b][:, :MEM], func=AF.Square),
                 R=[("ps", b)], W=[("sq", si)])
            b2 = bank()
            mm(b2, PS[b2][:, :MEM], blockones, sq_t[si][:, :MEM], True, True, R=[("sq", si), "cmat"])
            ri = nxt("rs", 2)
            rstd_from(b2, MEM, 0, ri)
            S.op("dve", lambda h, b=b, c2=c2, ri=ri: h.scalar_tensor_tensor(out=kmemT[:, c2 * MEM:(c2 + 1) * MEM], in0=PS[b][:, :MEM],
                                                                            scalar=cc(("mem_k", l)), in1=rs_t[ri][:, :MEM],
                                                                            op0=ALU.mult, op1=ALU.mult),
                 R=[("ps", b), ("rs", ri), "cst"], W=["kmemT"])
        for mc in range(2):
            b = bank()
            for kc in range(NCH):
                mm(b, PS[b][:, :256], memTn[:, kc * MEM + mc * 128: kc * MEM + (mc + 1) * 128], w[:, kc, 256:512], kc == 0, kc == NCH - 1,
                   R=["memTn", ("w", slot)])
            S.op("act", lambda h, b=b, mc=mc: h.copy(out=vmem[:, mc * 256:(mc + 1) * 256], in_=PS[b][:, :256]),
                 R=[("ps", b)], W=["vmem"])

    def mem_attn(l, n, qproj, qn_view):
        for c2 in range(2):
            b = bank()
            qproj(c2, b)
            si = nxt("sq", 2)
            S.op("act", lambda h, b=b, si=si: h.activation(out=sq_t[si][:, :n], in_=PS[b][:, :n], func=AF.Square),
                 R=[("ps", b)], W=[("sq", si)])
            b2 = bank()
            mm(b2, PS[b2][:, :n], blockones, sq_t[si][:, :n], True, True, R=[("sq", si), "cmat"])
            ri = nxt("rs", 2)
            rstd_from(b2, n, 0, ri)
            qn, qnk = qn_view(c2)
            S.op("dve", lambda h, b=b, ri=ri, qn=qn: h.scalar_tensor_tensor(out=qn, in0=PS[b][:, :n], scalar=cc(("mem_q", l)),
                                                                            in1=rs_t[ri][:, :n], op0=ALU.mult, op1=ALU.mult),
                 R=[("ps", b), ("rs", ri), "cst"], W=qnk)
            bo = bank()
            bd = bank()
            for hh in range(2):
                r0 = 64 * hh
                for mc in range(2):
                    bs = bank()
                    mm(bs, PS[bs][:, :n], kmemT[r0:r0 + 64, c2 * MEM + mc * 128: c2 * MEM + (mc + 1) * 128], qn[r0:r0 + 64, :],
                       True, True, R=qnk + ["kmemT"])
                    pi = nxt("pT", 4)
                    S.op("act", lambda h, bs=bs, pi=pi: h.activation(out=pT_t[pi][:, :n], in_=PS[bs][:, :n], func=AF.Exp, scale=0.125),
                         R=[("ps", bs)], W=[("pT", pi)])
                    hcol = (2 * c2 + hh) * 64
                    mm(bo, PS[bo][r0:r0 + 64, :n], vmem[:, mc * 256 + hcol: mc * 256 + hcol + 64], pT_t[pi][:, :n], mc == 0, mc == 1,
                       R=[("pT", pi), "vmem"])
                    mm(bd, PS[bd][r0:r0 + 64, :n], ones_b[:, 0:64], pT_t[pi][:, :n], mc == 0, mc == 1, R=[("pT", pi), "cmat"])
            S.op("dve", lambda h, bd=bd: h.reciprocal(rd_t[:, :n], PS[bd][:, :n]), R=[("ps", bd)], W=["rd"])
            S.op("dve", lambda h, bo=bo, c2=c2: h.tensor_tensor(out=mm_t[:, (6 + c2) * T:(6 + c2) * T + n], in0=PS[bo][:, :n], in1=rd_t[:, :n],
                                                                op=ALU.mult),
                 R=[("ps", bo), "rd"], W=[("mm", 6 + c2)])

    def wo_apply(n, slots, xap, xkey):
        for oc in range(NCH):
            b = bank()
            proj_chunk(b, n, slots, oc, lambda kc: (mm_t[:, kc * T: kc * T + n], [("mm", kc)]))
            S.op("dve", lambda h, b=b, oc=oc: h.tensor_tensor(out=xap(oc), in0=xap(oc), in1=PS[b][:, :n], op=ALU.add),
                 R=[("ps", b), xkey(oc)], W=[xkey(oc)])

    def mlp(l, tiles):
        for (n, xap, xkey, ti) in tiles:
            norm_tile(xap, xkey, n, ("norm_mlp", l), lambda c, ti=ti, n=n: h2_view(ti, c, n))
        for g in range(8):
            su = wload()
            sd = wload()
            for (n, xap, xkey, ti) in tiles:
                ai = nxt("a", 2)
                for oc in range(4):
                    b = bank()
                    for kc in range(NCH):
                        hap, hk = h2_view(ti, kc, n)
                        mm(b, PS[b][:, :n], wcol(su)[:, kc, oc * 128:(oc + 1) * 128], hap, kc == 0, kc == NCH - 1, R=hk + [("w", su)])
                    fi = nxt("sqf", 2)
                    S.op("act", lambda h, b=b, fi=fi, n=n: h.activation(out=sqf_t[fi][:, :n], in_=PS[b][:, :n], func=AF.Square),
                         R=[("ps", b)], W=[("sqf", fi)])
                    S.op("dve", lambda h, b=b, fi=fi, ai=ai, oc=oc, n=n: h.scalar_tensor_tensor(
                        out=a_t[ai][:, oc * T: oc * T + n], in0=PS[b][:, :n], scalar=0.0, in1=sqf_t[fi][:, :n], op0=ALU.is_gt, op1=ALU.mult),
                        R=[("ps", b), ("sqf", fi)], W=[("a", ai, oc)])
                for oc in range(NCH):
                    b = bank()
                    for kc in range(4):
                        mm(b, PS[b][:, :n], wdown(sd)[:, kc, oc * 128:(oc + 1) * 128], a_t[ai][:, kc * T: kc * T + n], kc == 0, kc == 3,
                           R=[("a", ai, kc), ("w", sd)])
                    S.op("dve", lambda h, b=b, oc=oc, xap=xap, n=n: h.tensor_tensor(out=xap(oc), in0=xap(oc), in1=PS[b][:, :n], op=ALU.add),
                         R=[("ps", b), xkey(oc)], W=[xkey(oc)])

    def a_layer(l):
        smk = wload()
        mem_kv(l, smk)
        win = [wload() for _ in range(5)]
        wo = [wload() for _ in range(2)]
        tiles = [(8, (lambda c: xh_t[:, c * 8:(c + 1) * 8]), (lambda c: ("xh", c)), "h")]
        for t in range(NT):
            xa, xk = xmain(t)
            tiles.append((T, xa, xk, t))
        if DEBUG is not None:
            tiles = [tt for tt in tiles if tt[3] in DEBUG["tiles"]]
        for (n, xap, xkey, ti) in tiles:
            hb = nxt("h", 2)
            norm_tile(xap, xkey, n, ("norm_mix", l), lambda c, hb=hb, n=n: h_view(hb, c, n))
            if DEBUG is not None and ti == 0 and l == 0:
                for c in range(NCH):
                    hap, hk = h_view(hb, c, n)
                    S.dma("sp", lambda h, c=c, hap=hap: h.dma_start(out=dbg_h[:, c * T:(c + 1) * T], in_=hap), R=hk, stream=f"dbg{c}")
            hin = lambda kc, hb=hb, n=n: h_view(hb, kc, n)
            for j in range(6):
                par = j % 2
                hvs, hvk = Rv(SCR + par * 2048, 2048, F32)
                u, uk = Rv(SCR + 4096 + par * 2560, 2560, F32)
                y, yk = Rv(SCR + 9216 + par * 2048, 2048, F32)
                bh = bank()
                proj_chunk(bh, n, win, 12 + j, hin)
                S.op("act", lambda h, bh=bh, hvs=hvs, n=n: h.copy(out=hvs[:, :n], in_=PS[bh][:, :n]), R=[("ps", bh)], W=hvk)
                bc = bank()
                proj_chunk(bc, n, win, 6 + j, hin)
                if ti == "h":
                    S.op("dve", lambda h, u=u: h.memset(u[:, 0:2], 0.0), W=uk)
                else:
                    kind = 1 if ti == 0 else (2 if ti == 2 else 0)
                    S.op("dve", lambda h, u=u, j=j, kind=kind: h.tensor_copy(u[:, 0:2], uh_t[:, (j * 3 + kind) * 2:(j * 3 + kind) * 2 + 2]),
                         R=[("uh", j)], W=uk)
                S.op("dve", lambda h, bc=bc, u=u, hvs=hvs, n=n: h.tensor_tensor(out=u[:, 2:2 + n], in0=PS[bc][:, :n], in1=hvs[:, :n], op=ALU.mult),
                     R=[("ps", bc)] + hvk, W=uk)
                if ti == "h":
                    S.op("dve", lambda h, u=u: h.tensor_tensor(out=u[:, 2:10], in0=u[:, 2:10], in1=hflag[:, 0:8], op=ALU.mult),
                         R=uk + ["hflag"], W=uk)
                    S.op("dve", lambda h, u=u, j=j: h.tensor_copy(uh_t[:, (j * 3 + 1) * 2:(j * 3 + 1) * 2 + 2], u[:, 4:6]), R=uk, W=[("uh", j)])
                    S.op("dve", lambda h, u=u, j=j: h.tensor_copy(uh_t[:, (j * 3 + 2) * 2:(j * 3 + 2) * 2 + 2], u[:, 8:10]), R=uk, W=[("uh", j)])
                else:
                    S.op("dve", lambda h, u=u, j=j, n=n: h.tensor_copy(uh_t[:, (j * 3) * 2:(j * 3) * 2 + 2], u[:, n:n + 2]), R=uk, W=[("uh", j)])
                S.op("act", lambda h, u=u, y=y, j=j, n=n: h.activation(out=y[:, :n], in_=u[:, 2:2 + n], func=AF.Copy, scale=cc(("conv", l, 2), j)),
                     R=uk + ["cst"], W=yk)
                S.op("dve", lambda h, u=u, y=y, j=j, n=n: h.scalar_tensor_tensor(out=y[:, :n], in0=u[:, 1:1 + n], scalar=cc(("conv", l, 1), j),
                                                                               in1=y[:, :n], op0=ALU.mult, op1=ALU.add),
                     R=uk + yk + ["cst"], W=yk)
                S.op("dve", lambda h, u=u, y=y, j=j, n=n: h.scalar_tensor_tensor(out=y[:, :n], in0=u[:, 0:n], scalar=cc(("conv", l, 0), j),
                                                                               in1=y[:, :n], op0=ALU.mult, op1=ALU.add),
                     R=uk + yk + ["cst"], W=yk)
                bg = bank()
                proj_chunk(bg, n, win, j, hin)
                S.op("dve", lambda h, bg=bg, y=y, j=j, n=n: h.tensor_tensor(out=mm_t[:, j * T: j * T + n], in0=PS[bg][:, :n], in1=y[:, :n], op=ALU.mult),
                     R=[("ps", bg)] + yk, W=[("mm", j)])
            mem_attn(l, n, lambda c2, bq, n=n, hin=hin: proj_chunk(bq, n, win, 18 + c2, hin),
                     lambda c2, n=n: (lambda v: (v[0][:, :n], v[1]))(Rv(SCR + 13312 + c2 * 1024, 1024)))
            if DEBUG is not None and ti == 0 and l == 0:
                S.dma("sp", lambda h: h.dma_start(out=dbg_mm, in_=mm_t[:]), R=[("mm", c) for c in range(NCH)], stream="dbgmm")
            wo_apply(n, wo, xap, xkey)
            if DEBUG is not None and ti == 0 and l == 0:
                S.dma("sp", lambda h: h.dma_start(out=dbg_x1.rearrange("p (c n) -> p c n", c=NCH), in_=x3[:, :, 0:T]), R=[("x", 0, c) for c in range(NCH)], stream="dbgx1")
        mlp(l, tiles_for_mlp(tiles))

    def tiles_for_mlp(tiles):
        out = []
        for (n, xap, xkey, ti) in tiles:
            out.append((n, xap, xkey, 4 if ti == "h" else ti))
        return out

    h2h_t = sb("h2h", [128, NCH * 8], BF16)
    _h2_view_orig = h2_view

    def h2_view(t, c, n=T):
        if t == 4:
            return h2h_t[:, c * 8: c * 8 + n], [("h2h", c)]
        return _h2_view_orig(t, c, n)

    def rope_tables(t, Cap, Ck, Sap, Sk, tmp, tmpk, ki, kik, posb, posk):
        S.dma("sp", lambda h: h.dma_start(out=posb, in_=pos_d[:, t * T:(t + 1) * T]), W=posk, stream="pos")
        TWO_PI = 2.0 * np.pi
        C1 = 6.28125
        C2 = TWO_PI - C1
        for which, (oap, ok) in enumerate(((Sap, Sk), (Cap, Ck))):
            if which == 0:
                S.op("dve", lambda h, oap=oap: h.tensor_scalar(out=oap, in0=posb, scalar1=cc("invf"), scalar2=None, op0=ALU.mult),
                     R=posk + ["cst"], W=ok)
            else:
                S.op("dve", lambda h, oap=oap: h.tensor_scalar(out=oap, in0=posb, scalar1=cc("invf"), scalar2=float(np.pi / 2), op0=ALU.mult, op1=ALU.add),
                     R=posk + ["cst"], W=ok)
            S.op("dve", lambda h, oap=oap: h.tensor_scalar(out=tmp, in0=oap, scalar1=float(1.0 / TWO_PI), scalar2=None, op0=ALU.mult),
                 R=ok, W=tmpk)
            S.op("dve", lambda h: h.tensor_copy(ki, tmp), R=tmpk, W=kik)
            S.op("dve", lambda h: h.tensor_copy(tmp, ki), R=kik, W=tmpk)
            S.op("dve", lambda h, oap=oap: h.scalar_tensor_tensor(out=oap, in0=tmp, scalar=-C1, in1=oap, op0=ALU.mult, op1=ALU.add),
                 R=tmpk + ok, W=ok)
            S.op("dve", lambda h, oap=oap: h.scalar_tensor_tensor(out=oap, in0=tmp, scalar=-C2, in1=oap, op0=ALU.mult, op1=ALU.add),
                 R=tmpk + ok, W=ok)
            S.op("dve", lambda h, oap=oap: h.tensor_scalar(out=oap, in0=oap, scalar1=3.1415925, scalar2=-3.1415925, op0=ALU.min, op1=ALU.max),
                 R=ok, W=ok)
            S.op("act", lambda h, oap=oap: h.activation(out=oap, in_=oap, func=AF.Sin), R=ok, W=ok)
        S.op("dve", lambda h: h.tensor_scalar(out=Sap, in0=Sap, scalar1=cc("sgn"), scalar2=None, op0=ALU.mult), R=Sk + ["cst"], W=Sk)

    def head_norm_rope(b, n, gname, Cap, Ck, Sap, Sk, out_ap, out_k, tq, tqk, tb, tbk):
        si = nxt("sq", 2)
        S.op("act", lambda h: h.activation(out=sq_t[si][:, :n], in_=PS[b][:, :n], func=AF.Square), R=[("ps", b)], W=[("sq", si)])
        b2 = bank()
        mm(b2, PS[b2][:, :n], blockones, sq_t[si][:, :n], True, True, R=[("sq", si), "cmat"])
        ri = nxt("rs", 2)
        rstd_from(b2, n, 0, ri)
        S.op("dve", lambda h: h.scalar_tensor_tensor(out=tq, in0=PS[b][:, :n], scalar=cc(gname), in1=rs_t[ri][:, :n], op0=ALU.mult, op1=ALU.mult),
             R=[("ps", b), ("rs", ri), "cst"], W=tqk)
        S.op("act", lambda h: h.copy(out=tb, in_=tq), R=tqk, W=tbk)
        b3 = bank()
        mm(b3, PS[b3][:, :n], perm_m, tb, True, True, R=tbk + ["cmat"])
        S.op("dve", lambda h: h.tensor_tensor(out=tq, in0=tq, in1=Cap, op=ALU.mult), R=tqk + Ck, W=tqk)
        fi = nxt("sqf", 2)
        S.op("dve", lambda h: h.tensor_tensor(out=sqf_t[fi][:, :n], in0=PS[b3][:, :n], in1=Sap, op=ALU.mult), R=[("ps", b3)] + Sk, W=[("sqf", fi)])
        S.op("dve", lambda h: h.tensor_tensor(out=out_ap, in0=tq, in1=sqf_t[fi][:, :n], op=ALU.add), R=tqk + [("sqf", fi)], W=out_k)

    def kv_stage():
        wk = [wload() for _ in range(3)]
        Cap, Ck = Rv(SCR, 2048, F32)
        Sap, Sk = Rv(SCR + 2048, 2048, F32)
        tmp, tmpk = Rv(SCR + 4096, 2048, F32)
        ki, kik = Rv(SCR + 6144, 2048, I32)
        posb, posk = Rv(SCR + 8192, 2048, F32)
        tq, tqk = Rv(SCR + 10240, 2048, F32)
        tb, tbk = Rv(SCR + 12288, 1024, BF16)
        for t in range(NT):
            xa, xk = xmain(t)
            hb = nxt("h", 2)
            norm_tile(xa, xk, T, "kv_norm", lambda c, hb=hb: h_view(hb, c, T))
            hin = lambda kc, hb=hb: h_view(hb, kc, T)
            rope_tables(t, Cap, Ck, Sap, Sk, tmp, tmpk, ki, kik, posb, posk)
            for hd in range(6):
                b = bank()
                proj_chunk(b, T, wk, hd, hin)
                ko, kok = Rv(SCR + 13312 + (hd % 2) * 1024, 1024, BF16)
                head_norm_rope(b, T, "k_norm", Cap, Ck, Sap, Sk, ko, kok, tq, tqk, tb, tbk)
                S.dma("sp", lambda h, hd=hd, ko=ko, t=t: h.dma_start(out=kT_d[hd, :, t * T:(t + 1) * T], in_=ko), R=kok, stream=f"ko{hd % 2}")
            for s4 in range(4):
                vo = a_t[s4 % 2]
                for (c0, cn, wsl, wc0) in ((0, 256, wk[1], 256), (256, 512, wk[2], 0)):
                    b = bank()
                    for kc in range(NCH):
                        hap, hk = hin(kc)
                        mm(b, PS[b][:, :cn], hap[:, s4 * 128:(s4 + 1) * 128], wcol(wsl)[:, kc, wc0:wc0 + cn], kc == 0, kc == NCH - 1,
                           R=hk + [("w", wsl)])
                    S.op("act", lambda h, b=b, vo=vo, c0=c0, cn=cn: h.copy(out=vo[:, c0:c0 + cn], in_=PS[b][:, :cn]),
                         R=[("ps", b)], W=[("a", s4 % 2, 0), ("a", s4 % 2, 1)])
                S.dma("sp", lambda h, vo=vo, t=t, s4=s4: h.dma_start(out=v_d[t * T + s4 * 128: t * T + (s4 + 1) * 128, :], in_=vo[:, 0:768]),
                      R=[("a", s4 % 2, 0), ("a", s4 % 2, 1)], stream=f"vo{s4 % 2}")

    def b_layer(l):
        j = l - 2
        smk = wload()
        mem_kv(l, smk)
        wq = [wload() for _ in range(2)]
        wo = [wload() for _ in range(2)]
        lam_init = 0.8 - 0.6 * float(np.exp(-0.3 * l))
        lt = lamt[:, j * 256:(j + 1) * 256]
        S.op("dve", lambda h: h.tensor_tensor(out=sqf_t[0][:, 0:64], in0=lt[:, 0:64], in1=lt[:, 64:128], op=ALU.mult), R=["lamt"], W=[("sqf", 0)])
        S.op("dve", lambda h: h.tensor_tensor(out=sqf_t[0][:, 64:128], in0=lt[:, 128:192], in1=lt[:, 192:256], op=ALU.mult), R=["lamt", ("sqf", 0)], W=[("sqf", 0)])
        S.op("dve", lambda h: h.tensor_reduce(out=lamv[:, 2:4], in_=sqf_t[0][:, 0:128].rearrange("p (a b) -> p a b", a=2), axis=mybir.AxisListType.X, op=ALU.add),
             R=[("sqf", 0)], W=["lamv"])
        S.op("act", lambda h: h.activation(out=lamv[:, 4:6], in_=lamv[:, 2:4], func=AF.Exp), R=["lamv"], W=["lamv"])
        S.op("dve", lambda h: h.tensor_tensor(out=lamv[:, 0:1], in0=lamv[:, 4:5], in1=lamv[:, 5:6], op=ALU.subtract), R=["lamv"], W=["lamv"])
        S.op("dve", lambda h: h.tensor_scalar(out=lamv[:, 1:2], in0=lamv[:, 0:1], scalar1=lam_init, scalar2=-1.0, op0=ALU.add, op1=ALU.mult), R=["lamv"], W=["lamv"])

        Cap, Ck = Rv(SCR, 2048, F32)
        Sap, Sk = Rv(SCR + 2048, 2048, F32)
        tmp, tmpk = Rv(SCR + 4096, 2048, F32)
        ki, kik = Rv(SCR + 6144, 2048, I32)
        posb, posk = Rv(SCR + 8192, 2048, F32)
        tq, tqk = Rv(SCR + 10240, 2048, F32)
        tb, tbk = Rv(SCR + 12288, 1024, BF16)
        qcb, qck = Rv(SCR + 13312, 2048, F32)
        tiles = []
        for t in range(NT):
            if DEBUG is not None and t not in DEBUG["tiles"]:
                continue
            xa, xk = xmain(t)
            tiles.append((T, xa, xk, t))
            hb = nxt("h", 2)
            norm_tile(xa, xk, T, ("norm_mix", l), lambda c, hb=hb: h_view(hb, c, T))
            hin = lambda kc, hb=hb: h_view(hb, kc, T)
            rope_tables(t, Cap, Ck, Sap, Sk, tmp, tmpk, ki, kik, posb, posk)
            S.dma("sp", lambda h, t=t: h.dma_start(out=qcb, in_=qc_d[:, t * T:(t + 1) * T]), W=qck, stream="qc")
            for hd in range(6):
                b = bank()
                proj_chunk(b, T, wq, hd, hin)
                head_norm_rope(b, T, ("b_q", j), Cap, Ck, Sap, Sk, qT_t[:, hd * T:(hd + 1) * T], [("qT", hd)], tq, tqk, tb, tbk)
            mem_attn(l, T, lambda c2, bq, hin=hin: proj_chunk(bq, T, wq, 6 + c2, hin),
                     lambda c2: Rv(SCR + 8192 + c2 * 1024, 1024))
            if DEBUG is not None:
                S.dma("sp", lambda h: h.dma_start(out=dbg_q, in_=qT_t[:]), R=[("qT", hd) for hd in range(6)], stream="dbgq")
            diff_attn(l, j, t, qcb, qck, lam_init)
            if DEBUG is not None:
                S.dma("sp", lambda h: h.dma_start(out=dbg_mm, in_=mm_t[:]), R=[("mm", c) for c in range(NCH)], stream="dbgmm")
            wo_apply(T, wo, xa, xk)
            if DEBUG is not None:
                S.dma("sp", lambda h, t=t: h.dma_start(out=dbg_x1.rearrange("p (c n) -> p c n", c=NCH), in_=x3[:, :, t * T:(t + 1) * T]), R=[("x", t, c) for c in range(NCH)], stream="dbgx1")
        mlp(l, tiles)

    def diff_attn(l, j, t, qcb, qck, lam_init):
        NSB = 8 if t < 2 else 16
        for hd in range(6):
            bO1, bO2, bD1, bD2 = 4, 5, 6, 7
            units = []

            def load_kv(sbk, ks):
                r = sbk if sbk < 8 else 15 - sbk
                half = 0 if sbk < 8 else 1
                S.dma("sp", lambda h, r=r, half=half, ks=ks, hd=hd: h.dma_start(out=kst[ks][:], in_=kTall_d[r, hd, :, half * BLK:(half + 1) * BLK]),
                      W=[("kst", ks)], stream=f"kst{ks}")
                S.dma("sp", lambda h, r=r, half=half, ks=ks, hd=hd: h.dma_start(
                    out=vst[ks][:].rearrange("p (b e) -> p b e", b=8),
                    in_=vall_d[r, half * BLK:(half + 1) * BLK, hd * 128:(hd + 1) * 128].rearrange("(b p) e -> p b e", p=128)),
                    W=[("vst", ks)], stream=f"vst{ks}")

            for sbk in range(NSB):
                ks = (hd * NSB + sbk) % 2
                for kb in range(8):
                    units.append((sbk, kb, ks))
            def front(u):
                sbk, kb, ks = units[u]
                if kb == 0:
                    load_kv(sbk, ks)
                kcol = sbk * 8 + kb
                bS = [(2 * u) % 4, (2 * u + 1) % 4]
                pis = []
                for m in range(2):
                    r0 = 64 * m
                    mm(bS[m], PS[bS[m]][:, :T], kst[ks][r0:r0 + 64, kb * 128:(kb + 1) * 128], qT_t[r0:r0 + 64, hd * T:(hd + 1) * T],
                       True, True, R=[("kst", ks), ("qT", hd)])
                for m in range(2):
                    pi = nxt("pT", 4)
                    pis.append(pi)
                    S.op("act", lambda h, m=m, pi=pi, bS=bS: h.activation(out=pT_t[pi][:], in_=PS[bS[m]][:, :T], func=AF.Exp, scale=0.125),
                         R=[("ps", bS[m])], W=[("pT", pi)])
                    if not (t >= 2 and sbk < 8):
                        S.op("dve", lambda h, pi=pi, kcol=kcol: h.scalar_tensor_tensor(out=pT_t[pi][:], in0=qcb, scalar=kc_t[:, kcol:kcol + 1],
                                                                                      in1=pT_t[pi][:], op0=ALU.is_ge, op1=ALU.mult),
                             R=qck + ["kc", ("pT", pi)], W=[("pT", pi)])
                return pis

            def back(u, pis):
                sbk, kb, ks = units[u]
                for m, (bo, bd) in enumerate(((bO1, bD1), (bO2, bD2))):
                    mm(bo, PS[bo][:, :T], vst[ks][:, kb * 128:(kb + 1) * 128], pT_t[pis[m]][:], u == 0, u == len(units) - 1, R=[("pT", pis[m]), ("vst", ks)])
                    mm(bd, PS[bd][:, :T], ones_b, pT_t[pis[m]][:], u == 0, u == len(units) - 1, R=[("pT", pis[m]), "cmat"])

            prev = front(0)
            for u in range(len(units)):
                nxtp = front(u + 1) if u + 1 < len(units) else None
                back(u, prev)
                prev = nxtp
            o1, o2 = osb[0], osb[1]
            S.op("dve", lambda h: h.reciprocal(rd_t[:], PS[bD1][:, :T]), R=[("ps", bD1)], W=["rd"])
            S.op("dve", lambda h: h.tensor_tensor(out=o1[:], in0=PS[bO1][:, :T], in1=rd_t[:], op=ALU.mult), R=[("ps", bO1), "rd"], W=[("osb", 0)])
            S.op("dve", lambda h: h.reciprocal(rd_t[:], PS[bD2][:, :T]), R=[("ps", bD2), ("osb", 0)], W=["rd"])
            S.op("dve", lambda h: h.tensor_tensor(out=o2[:], in0=PS[bO2][:, :T], in1=rd_t[:], op=ALU.mult), R=[("ps", bO2), "rd"], W=[("osb", 1)])
            S.op("dve", lambda h: h.scalar_tensor_tensor(out=o1[:], in0=o2[:], scalar=lamv[:, 1:2], in1=o1[:], op0=ALU.mult, op1=ALU.add),
                 R=[("osb", 0), ("osb", 1), "lamv"], W=[("osb", 0)])
            si = nxt("sq", 2)
            S.op("act", lambda h, si=si: h.activation(out=sq_t[si][:], in_=o1[:], func=AF.Square), R=[("osb", 0)], W=[("sq", si)])
            b2 = bank() % 4
            mm(b2, PS[b2][:, :T], ones_mean, sq_t[si][:], True, True, R=[("sq", si), "cmat"])
            ri = nxt("rs", 2)
            S.op("act", lambda h, b2=b2, ri=ri: h.activation(out=rs_t[ri][:], in_=PS[b2][:, :T], func=AF.Sqrt, bias=epsc[:, 1:2], scale=8.0),
                 R=[("ps", b2), "epsc"], W=[("rs", ri)])
            S.op("dve", lambda h, ri=ri: h.reciprocal(rs_t[ri][:], rs_t[ri][:]), R=[("rs", ri)], W=[("rs", ri)])
            S.op("dve", lambda h, ri=ri: h.scalar_tensor_tensor(out=o1[:], in0=o1[:], scalar=cc(("subln", j)), in1=rs_t[ri][:], op0=ALU.mult, op1=ALU.mult),
                 R=[("osb", 0), ("rs", ri), "cst"], W=[("osb", 0)])
            S.op("act", lambda h, hd=hd: h.activation(out=mm_t[:, hd * T:(hd + 1) * T], in_=o1[:], func=AF.Copy, scale=float(1.0 - lam_init)),
                 R=[("osb", 0)], W=[("mm", hd)])

    if phase == "A" and DEBUG is not None:
        dbg_h = dram("dbg_h", [128, NCH * T], BF16, "ExternalOutput")
        dbg_mm = dram("dbg_mm", [128, NCH * T], BF16, "ExternalOutput")
        dbg_x1 = dram("dbg_x1", [128, NCH * T], F32, "ExternalOutput")
        for l in DEBUG["layers"]:
            a_layer(l)
    elif phase == "A":
        a_layer(0)
        a_layer(1)
        kv_stage()
    elif DEBUG is not None:
        dbg_q = dram("dbg_q", [128, 6 * T], BF16, "ExternalOutput")
        dbg_mm = dram("dbg_mm", [128, NCH * T], BF16, "ExternalOutput")
        dbg_x1 = dram("dbg_x1", [128, NCH * T], F32, "ExternalOutput")
        for l in DEBUG["layers"]:
            b_layer(l)
    else:
        b_layer(2)
        b_layer(3)
    xout3 = xout_d.rearrange("p (c n) -> p c n", c=NCH)
    for t in range(NT):
        S.dma("sp", lambda h, t=t: h.dma_start(out=xout3[:, :, t * T:(t + 1) * T], in_=x3[:, :, t * T:(t + 1) * T]),
              R=[("x", t, c) for c in range(NCH)], stream="xo")
    outs = [o for o in S.ops["sp"] if o["stream"] is not None and (o["stream"].startswith("xo") or o["stream"].startswith("ko") or o["stream"].startswith("vo"))]
    last_by_stream = {}
    for o in outs:
        last_by_stream[o["stream"]] = o
    fin = S.op("sp", lambda h: h.nop(), R=(), W=())
    for o in last_by_stream.values():
        fin["deps"][o["id"]] = o

    S.finalize(None)
    sems = {}
    for e in S.order:
        sems[("e", e)] = es.enter_context(nc.semaphore(f"sem_{e}"))
    for sname in S.streams:
        sems[("s", sname)] = es.enter_context(nc.semaphore(f"ds_{sname}"))
    block = es.enter_context(nc.Block())
    S.emit(block, sems)
    es.close()
    return nc


def _pk(w):
    K, N = w.shape
    return np.ascontiguousarray(w.reshape(K // 128, 128, N).transpose(1, 0, 2))


def _halfslabs_cols(w):
    out = []
    for c0 in range(0, w.shape[1], 512):
        out.append(_pk(w[:, c0:c0 + 512]).reshape(128, HS))
    return out


def _halfslab_rows(w):
    return _pk(w).reshape(128, HS)


def _mlp_slabs(w_up, w_down):
    out = []
    for g in range(8):
        out.append(_pk(w_up[:, g * 512:(g + 1) * 512]).reshape(128, HS))
        out.append(_halfslab_rows(w_down[g * 512:(g + 1) * 512, :]))
    return out


def _col128(v):
    return np.ascontiguousarray(v.reshape(-1, 128).T)


def _build_cst(inp):
    cst = np.zeros((128, NCST), np.float32)
    for l in range(4):
        cst[:, CST[("norm_mix", l)]:CST[("norm_mix", l)] + 8] = _col128(inp["norm_mix"][l])
        cst[:, CST[("norm_mlp", l)]:CST[("norm_mlp", l)] + 8] = _col128(inp["norm_mlp"][l])
        cst[:, CST[("mem_q", l)]] = np.tile(inp["mem_q_norm"][l], 2)
        cst[:, CST[("mem_k", l)]] = np.tile(inp["mem_k_norm"][l], 2)
    cst[:, CST["kv_norm"]:CST["kv_norm"] + 8] = _col128(inp["kv_norm"])
    cst[:, CST["mem_norm"]:CST["mem_norm"] + 8] = _col128(inp["mem_norm"])
    for l in range(2):
        for k in range(3):
            cst[:, CST[("conv", l, k)]:CST[("conv", l, k)] + 6] = _col128(inp["a_conv"][l, k])
    for j in range(2):
        cst[:, CST[("b_q", j)]] = np.tile(inp["b_q_norm"][j], 2)
        cst[:, CST[("subln", j)]] = inp["b_subln"][j]
    cst[:, CST["k_norm"]] = np.tile(inp["k_norm"], 2)
    d = np.arange(128) % 64
    invf = np.where(d < 16, 1.0 / (np.float32(THETA) ** (np.arange(0, 16, 2, dtype=np.float32) / np.float32(16)))[d % 8], 0.0)
    cst[:, CST["invf"]] = invf.astype(np.float32)
    cst[:, CST["sgn"]] = np.where(d < 8, -1.0, np.where(d < 16, 1.0, 0.0))
    return cst


def _build_cmat():
    cm = np.zeros((128, 4 * 128), np.float32)
    cm[:, 0:128] = 1.0 / 1024.0
    for b in range(2):
        cm[b * 64:(b + 1) * 64, 128 + b * 64:128 + (b + 1) * 64] = 1.0 / 64.0
    cm[:, 256:384] = 1.0
    for m in range(128):
        d = m % 64
        if d < 8:
            cm[m + 8, 384 + m] = 1.0
        elif d < 16:
            cm[m - 8, 384 + m] = 1.0
    return cm


def _core_tokens(c):
    a = np.arange(c * BLK, (c + 1) * BLK)
    b = np.arange((15 - c) * BLK, (16 - c) * BLK)
    return np.concatenate([a, b])


def _qk_perm():
    idx = []
    for h in range(6):
        idx += list(range(h * 64, (h + 1) * 64))
        idx += list(range(384 + h * 64, 384 + (h + 1) * 64))
    return np.array(idx)


_PROGS = {}


def _prog(phase):
    if phase not in _PROGS:
        _PROGS[phase] = build_program(phase)
    return _PROGS[phase]


def _featmajor(xT_core):
    n = xT_core.shape[1]
    return np.ascontiguousarray(xT_core.reshape(8, 128, n).transpose(1, 0, 2).reshape(128, 8 * n))


def _unfeat(a, n):
    return a.reshape(128, 8, n).transpose(1, 0, 2).reshape(1024, n)


def _run_A(inp):
    x = inp["x"][0]
    xT = np.ascontiguousarray(x.T)
    memT = _featmajor(np.ascontiguousarray(inp["mem"][0].T))
    cst = _build_cst(inp)
    cmat = _build_cmat()
    perm = _qk_perm()

    wA = []
    for l in range(2):
        wA += _halfslabs_cols(inp["w_mem_kv"][l])
        wA += _halfslabs_cols(inp["a_w_in"][l])
        wA += _halfslabs_cols(inp["w_o"][l])
        wA += _mlp_slabs(inp["w_up"][l], inp["w_down"][l])
    wkv = inp["w_kv"]
    wkv_p = np.concatenate([wkv[:, :768][:, perm], wkv[:, 768:]], axis=1)
    wA += _halfslabs_cols(wkv_p)
    wA = np.stack(wA)
    in_maps = []
    toks = [_core_tokens(c) for c in range(NCORES)]
    for c in range(NCORES):
        tk = toks[c]
        xh = np.zeros((1024, 8), np.float32)
        hf = np.ones((128, 8), np.float32)
        for bi, b0 in enumerate((c * BLK, (15 - c) * BLK)):
            if b0 >= 4:
                xh[:, bi * 4:(bi + 1) * 4] = xT[:, b0 - 4:b0]
            else:
                hf[:, bi * 4:(bi + 1) * 4] = 0.0
        in_maps.append({
            "wts": wA, "cst": cst, "memT": memT, "cmat": cmat,
            "xin": _featmajor(xT[:, tk]), "pos": np.ascontiguousarray(np.broadcast_to(tk.astype(np.float32)[None, :], (128, NTOK))),
            "xh": _featmajor(xh), "hflag": hf,
        })
    resA = run_bass_kernel_spmd(_prog("A"), in_maps, core_ids=list(range(NCORES)))
    return resA.results


def _run_B(inp, RA, cores=None):
    memT = _featmajor(np.ascontiguousarray(inp["mem"][0].T))
    cst = _build_cst(inp)
    cmat = _build_cmat()
    perm = _qk_perm()
    toks = [_core_tokens(c) for c in range(NCORES)]
    wB = []
    for l in range(2, 4):
        j = l - 2
        wB += _halfslabs_cols(inp["w_mem_kv"][l])
        wq = inp["b_w_q"][j]
        wq_p = np.concatenate([wq[:, :768][:, perm], wq[:, 768:]], axis=1)
        wB += _halfslabs_cols(wq_p)
        wB += _halfslabs_cols(inp["w_o"][l])
        wB += _mlp_slabs(inp["w_up"][l], inp["w_down"][l])
    wB = np.stack(wB)
    kTall = np.stack([np.asarray(RA[c]["kT"]) for c in range(NCORES)])
    vall = np.stack([np.asarray(RA[c]["v"]) for c in range(NCORES)])
    lamb = np.ascontiguousarray(np.broadcast_to(inp["b_lam"].reshape(1, 512), (128, 512))).astype(np.float32)
    kc = np.zeros((128, 128), np.float32)
    for col in range(128):
        kc[:, col] = (col * 128 + np.arange(128)) // 64
    in_maps = []
    for c in range(NCORES):
        tk = toks[c]
        in_maps.append({
            "wts": wB, "cst": cst, "memT": memT, "cmat": cmat,
            "xin": np.asarray(RA[c]["xout"]), "pos": np.ascontiguousarray(np.broadcast_to(tk.astype(np.float32)[None, :], (128, NTOK))),
            "kTall": kTall, "vall": vall, "lamb": lamb,
            "qc": np.ascontiguousarray(np.broadcast_to((tk // 64).astype(np.float32)[None, :], (128, NTOK))),
            "kc": kc,
        })
    if cores is not None:
        res = run_bass_kernel_spmd(_prog("B"), [in_maps[c] for c in cores], core_ids=list(range(len(cores))))
        return res.results
    resB = run_bass_kernel_spmd(_prog("B"), in_maps, core_ids=list(range(NCORES)))
    out = np.zeros((SEQ, D), np.float32)
    for c in range(NCORES):
        out[toks[c], :] = _unfeat(np.asarray(resB.results[c]["xout"]), NTOK).T
    return out[None]


def kernel(**inp):
    inp = {k: np.asarray(v) for k, v in inp.items()}
    RA = _run_A(inp)
    return _run_B(inp, RA)
```
